# Optimizing a Trainium2 kernel written in Bass

```python
import math
import jax, jax.numpy as jnp
from jax import lax
import numpy as np

D_MODEL = 2048
BATCH = 2
SEQ = 4096
DEPTH = 1

MIX_WIDTH = D_MODEL
CONV_CH = MIX_WIDTH // 2
CONV_GROUPS = 8
CONV_WIDTH = 31
SG_CH = MIX_WIDTH - CONV_CH
SG_HEADS = 8
SG_HEAD_DIM = SG_CH // SG_HEADS
CHUNK = 128
D_FF = 5632
LN_EPS = 1e-5
DN_ALPHA = (2.0 * DEPTH) ** 0.25
DN_BETA = (8.0 * DEPTH) ** -0.25

kernel_name = "hybrid_conv_sgmlp_macaron_deepnorm"


def layer_norm(x, g, b):
    xf = x.astype(jnp.float32)
    mu = jnp.mean(xf, axis=-1, keepdims=True)
    var = jnp.mean(jnp.square(xf - mu), axis=-1, keepdims=True)
    y = (xf - mu) * lax.rsqrt(var + LN_EPS)
    return (y * g + b).astype(x.dtype)


def swiglu_ffn(x, w_gate_up, w_down):
    gu = x @ w_gate_up
    g, u = jnp.split(gu, 2, axis=-1)
    return (jax.nn.silu(g) * u) @ w_down


def conv_mixer(a_val, a_gate, conv_w, conv_b, ln_g, ln_b):
    h = a_val * jax.nn.sigmoid(a_gate)
    rhs = conv_w[:, None, :].astype(h.dtype)
    h = lax.conv_general_dilated(
        h, rhs, window_strides=(1,), padding=[(CONV_WIDTH - 1, 0)],
        dimension_numbers=("NWC", "WIO", "NWC"),
        feature_group_count=CONV_CH) + conv_b
    h = layer_norm(h, ln_g, ln_b)
    return jax.nn.silu(h)


def spatial_gating_mixer(b_u, b_v, ln_g, ln_b, w_s, b_s):
    bsz, seq, _ = b_u.shape
    n_chunks = seq // CHUNK
    u = jax.nn.gelu(b_u).reshape(bsz, n_chunks, CHUNK, SG_HEADS, SG_HEAD_DIM)
    v = jax.nn.gelu(b_v).reshape(bsz, n_chunks, CHUNK, SG_HEADS, SG_HEAD_DIM)
    v = layer_norm(v, ln_g, ln_b)
    causal = jnp.tril(jnp.ones((CHUNK, CHUNK), dtype=w_s.dtype))
    w = w_s * causal
    mixed = jnp.einsum("hts,bcshd->bcthd", w, v) + b_s.T[None, None, :, :, None]
    return (u * mixed).reshape(bsz, seq, SG_CH)


def setup_inputs(seed: int = 0) -> dict:
    key = jax.random.key(seed)
    ks = jax.random.split(key, 24)
    L = DEPTH
    nrm = lambda k, shape, s: jax.random.normal(k, shape, jnp.float32) * s
    gain = lambda k, shape: 1.0 + nrm(k, shape, 0.02)
    return {
        "x": nrm(ks[0], (BATCH, SEQ, D_MODEL), 1.0),
        "ffn1_w_gate_up": nrm(ks[1], (L, D_MODEL, 2 * D_FF), D_MODEL ** -0.5),
        "ffn1_w_down": nrm(ks[2], (L, D_FF, D_MODEL), DN_BETA * D_FF ** -0.5),
        "ln1_g": gain(ks[3], (L, D_MODEL)),
        "ln1_b": nrm(ks[4], (L, D_MODEL), 0.02),
        "mix_w_in": nrm(ks[5], (L, D_MODEL, 2 * CONV_CH + 2 * SG_CH), D_MODEL ** -0.5),
        "conv_w": nrm(ks[6], (L, CONV_WIDTH, CONV_CH), CONV_WIDTH ** -0.5),
        "conv_b": nrm(ks[7], (L, CONV_CH), 0.02),
        "conv_ln_g": gain(ks[8], (L, CONV_CH)),
        "conv_ln_b": nrm(ks[9], (L, CONV_CH), 0.02),
        "sg_ln_g": gain(ks[10], (L, SG_HEADS, SG_HEAD_DIM)),
        "sg_ln_b": nrm(ks[11], (L, SG_HEADS, SG_HEAD_DIM), 0.02),
        "sg_w": nrm(ks[12], (L, SG_HEADS, CHUNK, CHUNK), CHUNK ** -0.5),
        "sg_b": gain(ks[13], (L, SG_HEADS, CHUNK)),
        "mix_w_out": nrm(ks[14], (L, MIX_WIDTH, D_MODEL), DN_BETA * MIX_WIDTH ** -0.5),
        "ln2_g": gain(ks[15], (L, D_MODEL)),
        "ln2_b": nrm(ks[16], (L, D_MODEL), 0.02),
        "ffn2_w_gate_up": nrm(ks[17], (L, D_MODEL, 2 * D_FF), D_MODEL ** -0.5),
        "ffn2_w_down": nrm(ks[18], (L, D_FF, D_MODEL), DN_BETA * D_FF ** -0.5),
        "ln3_g": gain(ks[19], (L, D_MODEL)),
        "ln3_b": nrm(ks[20], (L, D_MODEL), 0.02),
    }


def reference(x, ffn1_w_gate_up, ffn1_w_down, ln1_g, ln1_b, mix_w_in, conv_w,
              conv_b, conv_ln_g, conv_ln_b, sg_ln_g, sg_ln_b, sg_w, sg_b,
              mix_w_out, ln2_g, ln2_b, ffn2_w_gate_up, ffn2_w_down, ln3_g, ln3_b):
    for l in range(DEPTH):
        x = layer_norm(DN_ALPHA * x + 0.5 * swiglu_ffn(x, ffn1_w_gate_up[l], ffn1_w_down[l]),
                       ln1_g[l], ln1_b[l])
        proj = x @ mix_w_in[l]
        a_val, a_gate, b_u, b_v = jnp.split(
            proj, [CONV_CH, 2 * CONV_CH, 2 * CONV_CH + SG_CH], axis=-1)
        y_a = conv_mixer(a_val, a_gate, conv_w[l], conv_b[l], conv_ln_g[l], conv_ln_b[l])
        y_b = spatial_gating_mixer(b_u, b_v, sg_ln_g[l], sg_ln_b[l], sg_w[l], sg_b[l])
        mix = jnp.concatenate([y_a, y_b], axis=-1) @ mix_w_out[l]
        x = layer_norm(DN_ALPHA * x + mix, ln2_g[l], ln2_b[l])
        x = layer_norm(DN_ALPHA * x + 0.5 * swiglu_ffn(x, ffn2_w_gate_up[l], ffn2_w_down[l]),
                       ln3_g[l], ln3_b[l])
    return x
```

```python
from contextlib import ExitStack
import numpy as np
import concourse.bass as bass
import concourse.mybir as mybir
from concourse.bass_utils import run_bass_kernel_spmd

F32 = mybir.dt.float32
BF16 = mybir.dt.bfloat16
AF = mybir.ActivationFunctionType
ALU = mybir.AluOpType
AX = mybir.AxisListType

NCORES = 8
D = 2048
DFF = 5632
TOK = 1024
HALO = 128
NT = TOK + HALO
KC = D // 128
NQ = 4
JQ = DFF // 128 // NQ
ALPHA = 2.0 ** 0.25
LN_EPS = 1e-5
ENGS = ["pe", "act", "dve", "pool", "sp"]
ESEM = {"pe": "pe", "act": "act", "dve": "dve", "pool": "pool"}


class Prog:
    def __init__(self):
        self.ops = {e: [] for e in ENGS}
        self.cnt = {}
        self.waited = {e: {} for e in ENGS}

    def newsem(self, name):
        self.cnt[name] = 0
        return name

    def emit(self, eng, fn, deps=(), sem=None, inc=1):
        waits = []
        for d in deps:
            if d is None:
                continue
            s, v = d
            if self.waited[eng].get(s, 0) >= v:
                continue
            self.waited[eng][s] = v
            waits.append((s, v))
        tok = None
        if sem is not None:
            self.cnt[sem] += inc
            tok = (sem, self.cnt[sem])
        self.ops[eng].append((waits, fn, sem, inc))
        return tok

    def build(self, nc, sems):
        handles = {"pe": "tensor", "act": "scalar", "dve": "vector", "pool": "gpsimd", "sp": "sync"}
        with nc.Block() as block:
            for e in ENGS:
                ops = self.ops[e]

                def body(engine, ops=ops):
                    for waits, fn, sem, inc in ops:
                        for s, v in waits:
                            engine.wait_ge(sems[s], v)
                        ins = fn(engine)
                        if sem is not None:
                            ins.then_inc(sems[sem], inc)

                getattr(block, handles[e])(body)


class Res:
    def __init__(self):
        self.w = None
        self.r = {}

    def rd(self):
        return [self.w]

    def wr(self):
        return [self.w] + [(s, v) for s, v in self.r.items()]

    def did_read(self, t):
        if t is not None:
            s, v = t
            self.r[s] = max(self.r.get(s, 0), v)

    def did_write(self, t):
        self.w = t
        self.r = {}


def build_program(debug=False):
    nc = bass.Bass("TRN2", target_bir_lowering=False)
    P = Prog()

    def din(name, shape):
        return nc.dram_tensor(name, list(shape), F32, kind="ExternalInput").ap()

    x_main = din("x_main", [TOK, D])
    x_halo = din("x_halo", [HALO, D])
    hmask = din("hmask", [128, 1])
    w_gu = [din("w_gu1", [D, 2 * DFF]), din("w_gu2", [D, 2 * DFF])]
    w_dn = [din("w_dn1", [DFF, D]), din("w_dn2", [DFF, D])]
    w_in = din("w_in", [D, 4096])
    w_out = din("w_out", [D, D])
    ln_g = [din(f"ln{i}_g", [1, D]) for i in (1, 2, 3)]
    ln_b = [din(f"ln{i}_b", [1, D]) for i in (1, 2, 3)]
    cw_d = din("cw", [128, 8 * 31])
    cvec_d = din("cvec", [128, 24])
    sgg_d = din("sgg", [1, 1024])
    sgb_d = din("sgbeta", [1, 1024])
    sgw_d = din("sgw", [8, 128, 128])
    sgbias_d = din("sgbias", [1, 1024])
    ident_d = din("ident", [128, 128])
    maskT_d = din("maskT", [128, 128])
    out_d = nc.dram_tensor("out", [TOK, D], F32, kind="ExternalOutput").ap()

    with ExitStack() as es:
        def sb(name, shape, dt):
            return es.enter_context(nc.sbuf_tensor("s_" + name, shape, dt))

        def ps(name):
            return es.enter_context(nc.psum_tensor(name, [128, 512], F32))

        R = [sb(f"R{t}", [128, D], F32) for t in range(8)]
        HX = sb("HX", [128, D], F32)
        XT = sb("XT", [128, KC, NT], BF16)
        HTr = sb("HTr", [128, JQ * NT], BF16)
        GBr = sb("GBr", [128, 2 * D], F32)
        TMPr = sb("TMPr", [128, 1024], F32)
        CO = sb("CO", [128, 8, 512], F32)
        WS = [sb(f"WS{i}", [128, 4096], BF16) for i in range(3)]
        ident = sb("ident", [128, 128], F32)
        maskT = sb("maskT", [128, 128], F32)
        ones32 = sb("ones32", [128, 128], F32)
        WmT = sb("WmT", [128, 8, 128], BF16)
        sgbias = sb("sgbias", [1, 1024], F32)
        cw = sb("cw", [128, 8 * 31], F32)
        cvec = sb("cvec", [128, 24], F32)
        hm = sb("hm", [128, 1], F32)
        st = sb("st", [128, 64], F32)
        epsln = sb("epsln", [128, 2], F32)

        hT = HTr[:, :].rearrange("p (j t) -> p j t", j=JQ)
        YT = HTr[:, 0:16 * 512].rearrange("p (k t) -> p k t", k=16)
        SGB = HTr[:, 16 * 512:16 * 512 + 2048].bitcast(F32)
        g_bc = GBr[:, 0:D]
        b_bc = GBr[:, D:2 * D]
        UT = [GBr[:, i * 512:(i + 1) * 512] for i in range(2)]
        V32 = [GBr[:, 1024 + i * 256:1024 + (i + 1) * 256] for i in range(2)]
        SQ32 = [GBr[:, 1536 + i * 256:1536 + (i + 1) * 256] for i in range(2)]
        SGG = GBr[:, 2048:3072]
        VBr = GBr[:, 3072:4096].bitcast(BF16)
        VB = [VBr[:, i * 1024:(i + 1) * 1024].rearrange("p (t d) -> p t d", t=4) for i in range(2)]
        HC = [HX[:, i * 640:(i + 1) * 640] for i in range(2)]
        sgw32 = CO[:, :, 0:128]

        PB = [ps(f"pb{i}") for i in range(8)]
        pbres = [Res() for _ in range(8)]

        semnames = ["pe", "act", "dve", "pool", "xld", "cst", "gb", "ost", "sgl", "w0", "w1", "w2"]
        for n in semnames:
            P.newsem(n)
        sems = {n: es.enter_context(nc.semaphore(n)) for n in semnames}

        rR = [Res() for _ in range(8)]
        rHX = Res()
        rXT = [Res() for _ in range(9)]
        rHT = Res()
        rGB = Res()
        rTMP = [Res(), Res()]
        rCO = Res()
        rWS = [Res() for _ in range(3)]
        rC = Res()
        rC2 = Res()
        rWm = Res()
        rST = Res()
        rUT = [Res(), Res()]
        rV32 = [Res(), Res()]
        rSQ = [Res(), Res()]
        rVB = [Res(), Res()]
        rHC = [Res(), Res()]
        rYT = Res()
        rSG = Res()

        def op(eng, fn, reads=(), writes=(), extra=()):
            deps = list(extra)
            for r in reads:
                deps += r.rd()
            for w in writes:
                deps += w.wr()
            tok = P.emit(eng, fn, deps, sem=ESEM[eng])
            for r in reads:
                r.did_read(tok)
            for w in writes:
                w.did_write(tok)
            return tok

        def dma(eng, out_ap, in_ap, sem, reads=(), writes=()):
            deps = []
            for r in reads:
                deps += r.rd()
            for w in writes:
                deps += w.wr()
            return P.emit(eng, lambda e: e.dma_start(out=out_ap, in_=in_ap), deps, sem=sem, inc=16)

        def mmgroup(bank, mms, reads):
            deps = pbres[bank].wr()
            for r in reads:
                deps += r.rd()
            tok = None
            n = len(mms)
            for i, (o, l, r_, st_, sp_) in enumerate(mms):
                tok = P.emit("pe", lambda e, o=o, l=l, r_=r_, st_=st_, sp_=sp_: e.matmul(o, l, r_, start=st_, stop=sp_),
                             deps if i == 0 else (), sem="pe" if i == n - 1 else None)
            pbres[bank].did_write(tok)
            for r in reads:
                r.did_read(tok)
            return tok

        gen_i = [0]

        def genbank():
            b = 4 + gen_i[0] % 4
            gen_i[0] += 1
            return b

        pair_i = [0]

        def pairbanks():
            s = pair_i[0] % 2
            pair_i[0] += 1
            return 2 * s, 2 * s + 1

        ws_i = [0]

        def wload(parts):
            s = ws_i[0] % 3
            ws_i[0] += 1
            deps = rWS[s].wr()
            tok = None
            for vf, src in parts:
                tok = P.emit("pool", lambda e, o=vf(WS[s]), i=src: e.dma_start(out=o, in_=i), deps, sem=f"w{s}", inc=16)
            rWS[s].did_write(tok)
            return s

        def wview(slot, k, n, off=0):
            return slot[:, off:off + k * n].rearrange("p (k n) -> p k n", k=k)

        ctoks = []
        for o, i in ((ident[:], ident_d), (maskT[:], maskT_d), (cw[:], cw_d), (cvec[:], cvec_d), (hm[:], hmask),
                     (sgbias[:], sgbias_d), (sgw32, sgw_d.rearrange("h t s -> t h s"))):
            ctoks.append(dma("sp", o, i, "cst"))
        ctok = ctoks[-1]
        rC.did_write(ctok)
        rCO.did_write(ctok)
        t1 = op("dve", lambda e: e.memset(ones32[:], 1.0), writes=[rC2])
        t2 = op("dve", lambda e: e.memset(epsln[:, 0:1], LN_EPS / (ALPHA * ALPHA)), writes=[rC2])
        t3 = op("dve", lambda e: e.memset(epsln[:, 1:2], LN_EPS), writes=[rC2])

        xtoks = []
        for t in range(8):
            xtoks.append(dma("sp", R[t][:], x_main[t * 128:(t + 1) * 128, :], "xld"))
        xtoks.append(dma("sp", HX[:], x_halo, "xld"))
        for t in range(8):
            rR[t].did_write(xtoks[-1])
        rHX.did_write(xtoks[-1])

        for h in range(8):
            b = genbank()
            mmgroup_tok = None
            deps = pbres[b].wr() + rC.rd() + rCO.rd()
            tk = P.emit("pe", lambda e, b=b, h=h: e.transpose(PB[b][:, 0:128], sgw32[:, h, :], ident[:]), deps, sem="pe")
            pbres[b].did_write(tk)
            rCO.did_read(tk)
            op("dve", lambda e, b=b, h=h: e.tensor_tensor(out=WmT[:, h, :], in0=PB[b][:, 0:128], in1=maskT[:], op=ALU.mult),
               reads=[pbres[b], rC], writes=[rWm])

        def rtile(t):
            if t == 0:
                return HX, rHX
            return R[t - 1], rR[t - 1]

        def transpose_tile(t):
            src, rs = rtile(t)
            for fb in range(4):
                b = genbank()
                deps = pbres[b].wr() + rs.rd() + rC.rd()
                tk = None
                for i in range(4):
                    f = fb * 4 + i
                    tk = P.emit("pe", lambda e, b=b, i=i, f=f, src=src: e.transpose(
                        PB[b][:, i * 128:(i + 1) * 128], src[:, f * 128:(f + 1) * 128], ident[:]),
                        deps if i == 0 else (), sem="pe" if i == 3 else None)
                pbres[b].did_write(tk)
                rs.did_read(tk)
                eng = "act" if fb % 2 == 0 else "dve"
                o = XT[:, fb * 4:fb * 4 + 4, t * 128:(t + 1) * 128]
                i_ = PB[b][:, :].rearrange("p (f n) -> p f n", f=4)
                if eng == "act":
                    op("act", lambda e, o=o, i_=i_: e.activation(out=o, in_=i_, func=AF.Copy), reads=[pbres[b]], writes=[rXT[t]])
                else:
                    op("dve", lambda e, o=o, i_=i_: e.tensor_copy(out=o, in_=i_), reads=[pbres[b]], writes=[rXT[t]])

        def xt_res(c0, c1):
            return [rXT[t] for t in range(c0 // 128, (c1 + 127) // 128)]

        def pair_block(slot, lA, lB, ncols, func, out_fn, out_res, ws_idx):
            for (c0, c1) in ncols:
                ba, bb = pairbanks()
                n = c1 - c0
                xr = xt_res(c0, c1)
                mmgroup(ba, [(PB[ba][:, 0:n], lA(k), XT[:, k, c0:c1], k == 0, k == KC - 1) for k in range(KC)],
                        reads=xr + [rWS[ws_idx]])
                mmgroup(bb, [(PB[bb][:, 0:n], lB(k), XT[:, k, c0:c1], k == 0, k == KC - 1) for k in range(KC)],
                        reads=xr + [rWS[ws_idx]])
                ti = pair_i[0] % 2
                tmp = TMPr[:, ti * 512:ti * 512 + n]
                op("act", lambda e, tmp=tmp, ba=ba, n=n: e.activation(out=tmp, in_=PB[ba][:, 0:n], func=func),
                   reads=[pbres[ba]], writes=[rTMP[ti]])
                op("dve", lambda e, tmp=tmp, bb=bb, n=n, o=out_fn(c0, c1): e.tensor_tensor(out=o, in0=tmp, in1=PB[bb][:, 0:n], op=ALU.mult),
                   reads=[pbres[bb], rTMP[ti]], writes=[out_res])

        def layer_norm(tiles, li, final=False):
            t_a = dma("sp", g_bc, ln_g[li].partition_broadcast(128), "gb", writes=[rGB])
            t_b = dma("sp", b_bc, ln_b[li].partition_broadcast(128), "gb", writes=[rGB])
            rGB.did_write(t_b)
            for t in tiles:
                src, rs = rtile(t)
                sc = (t % 4) * 16
                for i in range(4):
                    op("dve", lambda e, i=i, src=src: e.bn_stats(out=st[:, i * 6:(i + 1) * 6], in_=src[:, i * 512:(i + 1) * 512]),
                       reads=[rs], writes=[rST])
                op("dve", lambda e: e.bn_aggr(out=st[:, 24:26], in_=st[:, 0:24]), reads=[rST], writes=[rST])
                op("act", lambda e: e.activation(out=st[:, 26:27], in_=st[:, 25:26], func=AF.Sqrt, bias=epsln[:, 0:1], scale=1.0),
                   reads=[rST, rC2], writes=[rST])
                op("dve", lambda e: e.reciprocal(out=st[:, 27:28], in_=st[:, 26:27]), reads=[rST], writes=[rST])
                op("dve", lambda e, src=src: e.scalar_tensor_tensor(out=src[:, :], in0=src[:, :], scalar=st[:, 24:25], in1=g_bc,
                                                                     op0=ALU.subtract, op1=ALU.mult),
                   reads=[rST, rGB], writes=[rs])
                op("dve", lambda e, src=src: e.scalar_tensor_tensor(out=src[:, :], in0=src[:, :], scalar=st[:, 27:28], in1=b_bc,
                                                                     op0=ALU.mult, op1=ALU.add),
                   reads=[rST, rGB], writes=[rs])
                if not final:
                    transpose_tile(t)

        def ffn(l, tiles, li):
            c_lo = tiles[0] * 128
            c_hi = (tiles[-1] + 1) * 128
            ncols_tot = c_hi - c_lo
            nch = 3 if ncols_tot == NT else 2
            step = ncols_tot // nch
            ncols = [(c_lo + i * step, c_lo + (i + 1) * step) for i in range(nch)]
            wgu = w_gu[l].rearrange("(kc p) n -> p kc n", p=128)
            wdn = w_dn[l].rearrange("(j p) n -> p j n", p=128)
            coef = 0.5 / ALPHA
            for q in range(NQ):
                for jj in range(JQ):
                    j = q * JQ + jj
                    s = wload([(lambda sl: wview(sl, KC, 128, 0), wgu[:, :, j * 128:(j + 1) * 128]),
                               (lambda sl: wview(sl, KC, 128, 2048), wgu[:, :, DFF + j * 128:DFF + (j + 1) * 128])])
                    vg = wview(WS[s], KC, 128, 0)
                    vu = wview(WS[s], KC, 128, 2048)
                    pair_block(WS[s], lambda k, vg=vg: vg[:, k, :], lambda k, vu=vu: vu[:, k, :], ncols, AF.Silu,
                               lambda c0, c1, jj=jj: hT[:, jj, c0:c1], rHT, s)
                for nb in range(8):
                    s = wload([(lambda sl: wview(sl, JQ, 256, 0), wdn[:, q * JQ:(q + 1) * JQ, nb * 256:(nb + 1) * 256])])
                    vd = wview(WS[s], JQ, 256, 0)
                    for t in tiles:
                        b = genbank()
                        mmgroup(b, [(PB[b][:, 0:256], hT[:, jj, t * 128:(t + 1) * 128], vd[:, jj, :], jj == 0, jj == JQ - 1)
                                    for jj in range(JQ)], reads=[rHT, rWS[s]])
                        src, rs = rtile(t)
                        o = src[:, nb * 256:(nb + 1) * 256]
                        op("dve", lambda e, b=b, o=o: e.scalar_tensor_tensor(out=o, in0=PB[b][:, 0:256], scalar=coef, in1=o,
                                                                              op0=ALU.mult, op1=ALU.add),
                           reads=[pbres[b]], writes=[rs])
            return

        for t in range(9):
            transpose_tile(t)

        ffn(0, list(range(9)), 0)
        layer_norm(list(range(9)), 0)

        t_sg1 = dma("sp", SGG, sgg_d.partition_broadcast(128), "sgl", reads=[], writes=[rGB])
        t_sg2 = dma("sp", SGB, sgb_d.partition_broadcast(128), "sgl", reads=[], writes=[rHT])
        rSG.did_write(t_sg2)
        for r_ in rUT + rV32 + rSQ + rVB:
            r_.w = t_sg2
        rYT.w = t_sg2
        for r_ in rHC:
            r_.w = rHX.w
            r_.r = dict(rHX.r)
        win = w_in.rearrange("(kc p) n -> p kc n", p=128)
        wout = w_out.rearrange("(kc p) n -> p kc n", p=128)
        inv_a = 1.0 / ALPHA

        for g in range(2):
            gc0 = g * 512
            for c in range(8):
                s = wload([(lambda sl: wview(sl, KC, 128, 0), win[:, :, 1024 + c * 128:1024 + (c + 1) * 128]),
                           (lambda sl: wview(sl, KC, 128, 2048), win[:, :, c * 128:(c + 1) * 128])])
                vgt = wview(WS[s], KC, 128, 0)
                vvl = wview(WS[s], KC, 128, 2048)
                hi = c % 2
                hc = HC[hi]
                extra_res = rHX
                pair_block(WS[s], lambda k, v=vgt: v[:, k, :], lambda k, v=vvl: v[:, k, :],
                           [(gc0, gc0 + 320), (gc0 + 320, gc0 + 640)], AF.Sigmoid,
                           lambda c0, c1, hc=hc, gc0=gc0: hc[:, c0 - gc0:c1 - gc0], rHC[hi], s)
                if g == 0 and c < 2:
                    pass
                if g == 0:
                    op("dve", lambda e, hc=hc: e.tensor_scalar(out=hc[:, 0:128], in0=hc[:, 0:128], scalar1=hm[:, 0:1], scalar2=None,
                                                               op0=ALU.mult), reads=[rC], writes=[rHC[hi]])
                op("dve", lambda e, hc=hc, c=c: e.tensor_scalar(out=CO[:, c, :], in0=hc[:, 98:610], scalar1=cw[:, c * 31:c * 31 + 1],
                                                                 scalar2=cvec[:, c:c + 1], op0=ALU.mult, op1=ALU.add),
                   reads=[rHC[hi], rC], writes=[rCO])
                for k in range(1, 31):
                    op("dve", lambda e, hc=hc, c=c, k=k: e.scalar_tensor_tensor(out=CO[:, c, :], in0=hc[:, 98 + k:610 + k],
                                                                                 scalar=cw[:, c * 31 + k:c * 31 + k + 1], in1=CO[:, c, :],
                                                                                 op0=ALU.mult, op1=ALU.add),
                       reads=[rHC[hi]], writes=[rCO])
            for hp in range(4):
                s = wload([(lambda sl: wview(sl, KC, 256, 0), win[:, :, 3072 + hp * 256:3072 + (hp + 1) * 256])])
                vbv = wview(WS[s], KC, 256, 0)
                vi = hp % 2
                for tg in range(4):
                    tcol = 128 + (g * 4 + tg) * 128
                    b = genbank()
                    mmgroup(b, [(PB[b][:, 0:256], XT[:, k, tcol:tcol + 128], vbv[:, k, :], k == 0, k == KC - 1) for k in range(KC)],
                            reads=xt_res(tcol, tcol + 128) + [rWS[s]])
                    bi = tg % 2
                    v32 = V32[bi]
                    sq = SQ32[bi]
                    op("act", lambda e, b=b, v32=v32: e.activation(out=v32, in_=PB[b][:, 0:256], func=AF.Gelu_apprx_tanh),
                       reads=[pbres[b]], writes=[rV32[bi]])
                    op("act", lambda e, v32=v32, sq=sq: e.activation(out=sq, in_=v32, func=AF.Square),
                       reads=[rV32[bi]], writes=[rSQ[bi]])
                    op("dve", lambda e, v32=v32: e.tensor_reduce(out=st[:, 32:34], in_=v32.rearrange("p (h d) -> p h d", h=2),
                                                                 axis=AX.X, op=ALU.add), reads=[rV32[bi]], writes=[rST])
                    op("dve", lambda e, sq=sq: e.tensor_reduce(out=st[:, 34:36], in_=sq.rearrange("p (h d) -> p h d", h=2),
                                                               axis=AX.X, op=ALU.add), reads=[rSQ[bi]], writes=[rST])
                    op("dve", lambda e: e.tensor_scalar(out=st[:, 36:38], in0=st[:, 32:34], scalar1=1.0 / 128, scalar2=None, op0=ALU.mult),
                       reads=[rST], writes=[rST])
                    op("dve", lambda e: e.tensor_tensor(out=st[:, 38:40], in0=st[:, 36:38], in1=st[:, 36:38], op=ALU.mult),
                       reads=[rST], writes=[rST])
                    op("dve", lambda e: e.scalar_tensor_tensor(out=st[:, 40:42], in0=st[:, 34:36], scalar=1.0 / 128, in1=st[:, 38:40],
                                                               op0=ALU.mult, op1=ALU.subtract), reads=[rST], writes=[rST])
                    op("act", lambda e: e.activation(out=st[:, 42:44], in_=st[:, 40:42], func=AF.Sqrt, bias=epsln[:, 1:2], scale=1.0),
                       reads=[rST, rC2], writes=[rST])
                    op("dve", lambda e: e.reciprocal(out=st[:, 44:46], in_=st[:, 42:44]), reads=[rST], writes=[rST])
                    op("dve", lambda e: e.scalar_tensor_tensor(out=st[:, 46:48], in0=st[:, 36:38], scalar=-1.0, in1=st[:, 44:46],
                                                               op0=ALU.mult, op1=ALU.mult), reads=[rST], writes=[rST])
                    for hh in range(2):
                        op("act", lambda e, v32=v32, hh=hh: e.activation(out=v32[:, hh * 128:(hh + 1) * 128], in_=v32[:, hh * 128:(hh + 1) * 128],
                                                                         func=AF.Identity, bias=st[:, 46 + hh:47 + hh], scale=st[:, 44 + hh:45 + hh]),
                           reads=[rST], writes=[rV32[bi]])
                    gsl = slice(hp * 256, (hp + 1) * 256)
                    op("dve", lambda e, v32=v32, gsl=gsl: e.tensor_tensor(out=v32, in0=v32, in1=SGG[:, gsl], op=ALU.mult),
                       reads=[rSG], writes=[rV32[bi]])
                    op("dve", lambda e, v32=v32, gsl=gsl, vi=vi, tg=tg: e.tensor_tensor(out=VB[vi][:, tg, :], in0=v32, in1=SGB[:, gsl], op=ALU.add),
                       reads=[rSG, rV32[bi]], writes=[rVB[vi]])
                s2 = wload([(lambda sl: wview(sl, KC, 256, 0), win[:, :, 2048 + hp * 256:2048 + (hp + 1) * 256])])
                vbu = wview(WS[s2], KC, 256, 0)
                for hh in range(2):
                    h = hp * 2 + hh
                    b = genbank()
                    mc0 = 128 + g * 512
                    mmgroup(b, [(PB[b][:, 0:512], vbu[:, k, hh * 128:(hh + 1) * 128], XT[:, k, mc0:mc0 + 512], k == 0, k == KC - 1)
                                for k in range(KC)], reads=xt_res(mc0, mc0 + 512) + [rWS[s2]])
                    ui = h % 2
                    op("act", lambda e, b=b, ui=ui: e.activation(out=UT[ui], in_=PB[b][:, 0:512], func=AF.Gelu_apprx_tanh),
                       reads=[pbres[b]], writes=[rUT[ui]])
                    b2 = genbank()
                    mms = []
                    for tg in range(4):
                        o = PB[b2][:, tg * 128:(tg + 1) * 128]
                        mms.append((o, VB[vi][:, tg, hh * 128:(hh + 1) * 128], WmT[:, h, :], True, False))
                        mms.append((o, ones32[0:1, 0:128], sgbias[0:1, h * 128:(h + 1) * 128], False, True))
                    mmgroup(b2, mms, reads=[rVB[vi], rWm, rC, rC2])
                    op("dve", lambda e, b2=b2, ui=ui, h=h: e.tensor_tensor(out=YT[:, 8 + h, :], in0=PB[b2][:, 0:512], in1=UT[ui], op=ALU.mult),
                       reads=[pbres[b2], rUT[ui]], writes=[rYT])
            bs = genbank()
            mms = []
            for tg in range(4):
                for c in range(8):
                    mms.append((PB[bs][:, 2 * tg:2 * tg + 2], CO[:, c, tg * 128:(tg + 1) * 128], ones32[:, 0:2], c == 0, c == 7))
            mmgroup(bs, mms, reads=[rCO, rC2])
            bq = genbank()
            first = True
            for tg in range(4):
                for c in range(8):
                    qi = (tg * 8 + c) % 2
                    sqc = TMPr[:, qi * 512:qi * 512 + 128]
                    op("act", lambda e, sqc=sqc, c=c, tg=tg: e.activation(out=sqc, in_=CO[:, c, tg * 128:(tg + 1) * 128], func=AF.Square),
                       reads=[rCO], writes=[rTMP[qi]])
                    mmgroup(bq, [(PB[bq][:, 2 * tg:2 * tg + 2], sqc, ones32[:, 0:2], c == 0, c == 7)], reads=[rTMP[qi], rC2])
            op("dve", lambda e, bs=bs: e.tensor_scalar(out=st[:, 48:52], in0=PB[bs][:, 0:8].rearrange("p (t two) -> p t two", two=2)[:, :, 0], scalar1=1.0 / 1024, scalar2=None, op0=ALU.mult),
               reads=[pbres[bs]], writes=[rST])
            op("dve", lambda e: e.tensor_tensor(out=st[:, 52:56], in0=st[:, 48:52], in1=st[:, 48:52], op=ALU.mult), reads=[rST], writes=[rST])
            op("dve", lambda e, bq=bq: e.scalar_tensor_tensor(out=st[:, 56:60], in0=PB[bq][:, 0:8].rearrange("p (t two) -> p t two", two=2)[:, :, 0], scalar=1.0 / 1024, in1=st[:, 52:56],
                                                              op0=ALU.mult, op1=ALU.subtract), reads=[pbres[bq], rST], writes=[rST])
            op("act", lambda e: e.activation(out=st[:, 56:60], in_=st[:, 56:60], func=AF.Sqrt, bias=epsln[:, 1:2], scale=1.0),
               reads=[rST, rC2], writes=[rST])
            op("dve", lambda e: e.reciprocal(out=st[:, 52:56], in_=st[:, 56:60]), reads=[rST], writes=[rST])
            op("dve", lambda e: e.scalar_tensor_tensor(out=st[:, 56:60], in0=st[:, 48:52], scalar=-1.0, in1=st[:, 52:56],
                                                       op0=ALU.mult, op1=ALU.mult), reads=[rST], writes=[rST])
            b_r = genbank()
            b_m = genbank()
            for tg in range(4):
                for which, bb_, col in ((0, b_r, 52), (1, b_m, 56)):
                    qi = (tg * 2 + which) % 2
                    dg = TMPr[:, qi * 512:qi * 512 + 128]
                    op("dve", lambda e, dg=dg, col=col, tg=tg: e.tensor_scalar(out=dg, in0=ident[:], scalar1=st[:, col + tg:col + tg + 1],
                                                                                scalar2=None, op0=ALU.mult),
                       reads=[rST, rC], writes=[rTMP[qi]])
                    mmgroup(bb_, [(PB[bb_][:, tg * 128:(tg + 1) * 128], ones32[:, :], dg, True, True)], reads=[rTMP[qi], rC2])
            for c in range(8):
                op("dve", lambda e, c=c, b_r=b_r: e.tensor_tensor(out=CO[:, c, :], in0=CO[:, c, :], in1=PB[b_r][:, 0:512], op=ALU.mult),
                   reads=[pbres[b_r]], writes=[rCO])
                op("dve", lambda e, c=c, b_m=b_m: e.tensor_tensor(out=CO[:, c, :], in0=CO[:, c, :], in1=PB[b_m][:, 0:512], op=ALU.add),
                   reads=[pbres[b_m]], writes=[rCO])
                op("act", lambda e, c=c: e.activation(out=YT[:, c, :], in_=CO[:, c, :], func=AF.Silu,
                                                      bias=cvec[:, 16 + c:17 + c], scale=cvec[:, 8 + c:9 + c]),
                   reads=[rCO, rC], writes=[rYT])
            for nb in range(8):
                s = wload([(lambda sl: wview(sl, KC, 256, 0), wout[:, :, nb * 256:(nb + 1) * 256])])
                vo = wview(WS[s], KC, 256, 0)
                for tg in range(4):
                    t = 1 + g * 4 + tg
                    b = genbank()
                    mmgroup(b, [(PB[b][:, 0:256], YT[:, kk, tg * 128:(tg + 1) * 128], vo[:, kk, :], kk == 0, kk == KC - 1)
                                for kk in range(KC)], reads=[rYT, rWS[s]])
                    src, rs = rtile(t)
                    o = src[:, nb * 256:(nb + 1) * 256]
                    op("dve", lambda e, b=b, o=o: e.scalar_tensor_tensor(out=o, in0=PB[b][:, 0:256], scalar=inv_a, in1=o,
                                                                          op0=ALU.mult, op1=ALU.add),
                       reads=[pbres[b]], writes=[rs])
        for r_ in rUT + rV32 + rSQ + rVB + [rSG]:
            for tk in r_.wr():
                rGB.did_read(tk)
        for tk in rYT.wr() + rSG.wr():
            rHT.did_read(tk)

        layer_norm(list(range(1, 9)), 1)
        ffn(1, list(range(1, 9)), 2)
        layer_norm(list(range(1, 9)), 2, final=True)

        otok = None
        for t in range(8):
            otok = dma("sp", out_d[t * 128:(t + 1) * 128, :], R[t][:], "ost", reads=[rR[t]])
        P.emit("sp", lambda e: e.nop(), [otok])

        P.build(nc, sems)
    return nc


_NC_CACHE = {}


def kernel(x, ffn1_w_gate_up, ffn1_w_down, ln1_g, ln1_b, mix_w_in, conv_w, conv_b, conv_ln_g, conv_ln_b,
           sg_ln_g, sg_ln_b, sg_w, sg_b, mix_w_out, ln2_g, ln2_b, ffn2_w_gate_up, ffn2_w_down, ln3_g, ln3_b):
    f = lambda a: np.ascontiguousarray(np.asarray(a, dtype=np.float32))
    x = f(x)
    common = {
        "w_gu1": f(ffn1_w_gate_up)[0], "w_gu2": f(ffn2_w_gate_up)[0],
        "w_dn1": f(ffn1_w_down)[0], "w_dn2": f(ffn2_w_down)[0],
        "w_in": f(mix_w_in)[0], "w_out": f(mix_w_out)[0],
        "ln1_g": f(ln1_g).reshape(1, D), "ln1_b": f(ln1_b).reshape(1, D),
        "ln2_g": f(ln2_g).reshape(1, D), "ln2_b": f(ln2_b).reshape(1, D),
        "ln3_g": f(ln3_g).reshape(1, D), "ln3_b": f(ln3_b).reshape(1, D),
        "cw": np.ascontiguousarray(f(conv_w)[0].reshape(31, 8, 128).transpose(2, 1, 0).reshape(128, 8 * 31)),
        "cvec": np.ascontiguousarray(np.concatenate([f(conv_b)[0].reshape(8, 128).T, f(conv_ln_g)[0].reshape(8, 128).T,
                                                     f(conv_ln_b)[0].reshape(8, 128).T], axis=1)),
        "sgg": f(sg_ln_g).reshape(1, 1024), "sgbeta": f(sg_ln_b).reshape(1, 1024),
        "sgw": f(sg_w)[0], "sgbias": f(sg_b).reshape(1, 1024),
        "ident": np.eye(128, dtype=np.float32),
        "maskT": np.triu(np.ones((128, 128), dtype=np.float32)),
    }
    in_maps = []
    for c in range(NCORES):
        b, s0 = c // 4, (c % 4) * TOK
        m = dict(common)
        m["x_main"] = np.ascontiguousarray(x[b, s0:s0 + TOK])
        if s0 > 0:
            m["x_halo"] = np.ascontiguousarray(x[b, s0 - HALO:s0])
            m["hmask"] = np.ones((128, 1), np.float32)
        else:
            m["x_halo"] = np.zeros((HALO, D), np.float32)
            m["hmask"] = np.zeros((128, 1), np.float32)
        in_maps.append(m)
    if "nc" not in _NC_CACHE:
        _NC_CACHE["nc"] = build_program()
    nc = _NC_CACHE["nc"]
    res = run_bass_kernel_spmd(nc, in_maps, core_ids=list(range(NCORES)))
    out = np.empty((2, 4096, D), np.float32)
    for c in range(NCORES):
        b, s0 = c // 4, (c % 4) * TOK
        out[b, s0:s0 + TOK] = res.results[c]["out"]
    return out
```

```python
from contextlib import ExitStack
import numpy as np
import concourse.bass as bass
import concourse.mybir as mybir
from concourse.bass_utils import run_bass_kernel_spmd

F32 = mybir.dt.float32
F32R = mybir.dt.float32r
BF16 = mybir.dt.bfloat16
AF = mybir.ActivationFunctionType
ALU = mybir.AluOpType
AX = mybir.AxisListType

NCORES = 8
D = 2048
DFF = 5632
TOK = 1024
HALO = 128
NT = TOK + HALO
KC = D // 128
NQ = 4
JQ = DFF // 128 // NQ
ALPHA = 2.0 ** 0.25
LN_EPS = 1e-5
ENGS = ["pe", "act", "dve", "pool", "sp"]
ESEM = {"pe": "pe", "act": "act", "dve": "dve", "pool": "pool"}


class Prog:
    def __init__(self):
        self.ops = {e: [] for e in ENGS}
        self.cnt = {}
        self.waited = {e: {} for e in ENGS}

    def newsem(self, name):
        self.cnt[name] = 0
        return name

    def emit(self, eng, fn, deps=(), sem=None, inc=1):
        waits = []
        for d in deps:
            if d is None:
                continue
            s, v = d
            if self.waited[eng].get(s, 0) >= v:
                continue
            self.waited[eng][s] = v
            waits.append((s, v))
        tok = None
        if sem is not None:
            self.cnt[sem] += inc
            tok = (sem, self.cnt[sem])
        self.ops[eng].append((waits, fn, sem, inc))
        return tok

    def build(self, nc, sems):
        handles = {"pe": "tensor", "act": "scalar", "dve": "vector", "pool": "gpsimd", "sp": "sync"}
        with nc.Block() as block:
            for e in ENGS:
                ops = self.ops[e]

                def body(engine, ops=ops):
                    for waits, fn, sem, inc in ops:
                        for s, v in waits:
                            engine.wait_ge(sems[s], v)
                        ins = fn(engine)
                        if sem is not None:
                            ins.then_inc(sems[sem], inc)

                getattr(block, handles[e])(body)


class Res:
    def __init__(self):
        self.w = None
        self.r = {}

    def rd(self):
        return [self.w]

    def wr(self):
        return [self.w] + [(s, v) for s, v in self.r.items()]

    def did_read(self, t):
        if t is not None:
            s, v = t
            self.r[s] = max(self.r.get(s, 0), v)

    def did_write(self, t):
        self.w = t
        self.r = {}


def build_program(debug=False):
    nc = bass.Bass("TRN2", target_bir_lowering=False)
    P = Prog()

    def din(name, shape):
        return nc.dram_tensor(name, list(shape), F32, kind="ExternalInput").ap()

    x_main = din("x_main", [TOK, D])
    x_halo = din("x_halo", [HALO, D])
    hmask = din("hmask", [128, 1])
    w_gu = [din("w_gu1", [D, 2 * DFF]), din("w_gu2", [D, 2 * DFF])]
    w_dn = [din("w_dn1", [DFF, D]), din("w_dn2", [DFF, D])]
    w_in = din("w_in", [D, 4096])
    w_out = din("w_out", [D, D])
    ln_g = [din(f"ln{i}_g", [1, D]) for i in (1, 2, 3)]
    ln_b = [din(f"ln{i}_b", [1, D]) for i in (1, 2, 3)]
    cw_d = din("cw", [128, 8 * 31])
    cvec_d = din("cvec", [128, 24])
    sgT_d = din("sgT", [128, 16])
    sgw_d = din("sgw", [8, 128, 128])
    sgbias_d = din("sgbias", [1, 1024])
    ident_d = din("ident", [128, 128])
    maskT_d = din("maskT", [128, 128])
    out_d = nc.dram_tensor("out", [TOK, D], F32, kind="ExternalOutput").ap()

    with ExitStack() as es:
        def sb(name, shape, dt):
            return es.enter_context(nc.sbuf_tensor("s_" + name, shape, dt))

        def ps(name):
            return es.enter_context(nc.psum_tensor(name, [128, 512], F32))

        R = [sb(f"R{t}", [128, D], F32) for t in range(8)]
        HX = sb("HX", [128, D], F32)
        XT = sb("XT", [128, KC, NT], BF16)
        HTr = sb("HTr", [128, JQ * NT], BF16)
        GBr = sb("GBr", [128, 2 * D], F32)
        HCr = sb("HCr", [128, 1280], F32)
        BANK = sb("BANK", [128, 31 * 128], F32)
        WS = [sb(f"WS{i}", [128, 4096], BF16) for i in range(3)]
        ident = sb("ident", [128, 128], F32)
        maskT = sb("maskT", [128, 128], F32)
        ones32 = sb("ones32", [128, 128], F32)
        WmT = sb("WmT", [128, 8, 128], BF16)
        BB2 = sb("BB2", [128, 8, 128], F32)
        sgT = sb("sgT", [128, 16], F32)
        ones_bf = sb("ones_bf", [128, 128], BF16)
        junk = sb("junk", [128, 128], F32)
        cw = sb("cw", [128, 8 * 31], F32)
        cvec = sb("cvec", [128, 24], F32)
        hm = sb("hm", [128, 1], F32)
        st = sb("st", [128, 128], F32)
        epsln = sb("epsln", [128, 2], F32)

        hT = HTr[:, :].rearrange("p (j t) -> p j t", j=JQ)
        YT = HTr[:, 0:16 * 512].rearrange("p (k t) -> p k t", k=16)
        g_bc = GBr[:, 0:D]
        b_bc = GBr[:, D:2 * D]
        TMPr = GBr[:, 3072:4096]
        VBr = GBr[:, 2048:3072].bitcast(BF16)
        VB = [VBr[:, i * 1024:(i + 1) * 1024].rearrange("p (t d) -> p t d", t=4) for i in range(2)]
        UT = [HX[:, i * 512:(i + 1) * 512] for i in range(2)]
        V32 = [HX[:, 1024 + i * 256:1024 + (i + 1) * 256] for i in range(4)]
        HC = [HCr[:, i * 640:(i + 1) * 640] for i in range(2)]
        bank = BANK[:, :].rearrange("p (k n) -> p k n", k=31)
        sgw32 = GBr[:, 0:1024].rearrange("p (h s) -> p h s", h=8)
        bbias = GBr[:, 1024:2048].rearrange("p (h t) -> p h t", h=8)

        def COv(c):
            if c < 4:
                return GBr[:, c * 512:(c + 1) * 512]
            return HTr[:, 8192 + (c - 4) * 1024:8192 + (c - 3) * 1024].bitcast(F32)

        PB = [ps(f"pb{i}") for i in range(8)]
        pbres = [Res() for _ in range(8)]

        semnames = ["pe", "act", "dve", "pool", "xld", "cst", "gb", "ost", "sgl", "w0", "w1", "w2"]
        for n in semnames:
            P.newsem(n)
        sems = {n: es.enter_context(nc.semaphore(n)) for n in semnames}

        rR = [Res() for _ in range(8)]
        rHX = Res()
        rXT = [Res() for _ in range(9)]
        rHT = Res()
        rGB = Res()
        rTMP = [Res(), Res()]
        rCO = Res()
        rWS = [Res() for _ in range(3)]
        rC = Res()
        rC2 = Res()
        rWm = Res()
        rST = Res()
        rUT = [Res(), Res()]
        rV32 = [Res() for _ in range(4)]
        rJ = Res()
        rSQ = [Res(), Res()]
        rVB = [Res(), Res()]
        rHC = [Res(), Res()]
        rYT = Res()

        def inherit(dst, *owners):
            for o in owners:
                for tk in o.wr():
                    dst.did_read(tk)

        def op(eng, fn, reads=(), writes=(), extra=()):
            deps = list(extra)
            for r in reads:
                deps += r.rd()
            for w in writes:
                deps += w.wr()
            tok = P.emit(eng, fn, deps, sem=ESEM[eng])
            for r in reads:
                r.did_read(tok)
            for w in writes:
                w.did_write(tok)
            return tok

        def dma(eng, out_ap, in_ap, sem, reads=(), writes=()):
            deps = []
            for r in reads:
                deps += r.rd()
            for w in writes:
                deps += w.wr()
            return P.emit(eng, lambda e: e.dma_start(out=out_ap, in_=in_ap), deps, sem=sem, inc=16)

        def mmgroup(bank, mms, reads):
            deps = pbres[bank].wr()
            for r in reads:
                deps += r.rd()
            tok = None
            n = len(mms)
            for i, (o, l, r_, st_, sp_) in enumerate(mms):
                tok = P.emit("pe", lambda e, o=o, l=l, r_=r_, st_=st_, sp_=sp_: e.matmul(o, l, r_, start=st_, stop=sp_),
                             deps if i == 0 else (), sem="pe" if i == n - 1 else None)
            pbres[bank].did_write(tok)
            for r in reads:
                r.did_read(tok)
            return tok

        gen_i = [0]

        def genbank():
            b = 4 + gen_i[0] % 4
            gen_i[0] += 1
            return b

        pair_i = [0]

        def pairbanks():
            s = pair_i[0] % 2
            pair_i[0] += 1
            return 2 * s, 2 * s + 1

        ws_i = [0]

        def wload(parts):
            s = ws_i[0] % 3
            ws_i[0] += 1
            deps = rWS[s].wr()
            tok = None
            for vf, src in parts:
                tok = P.emit("pool", lambda e, o=vf(WS[s]), i=src: e.dma_start(out=o, in_=i), deps, sem=f"w{s}", inc=16)
            rWS[s].did_write(tok)
            return s

        def wview(slot, k, n, off=0):
            return slot[:, off:off + k * n].rearrange("p (k n) -> p k n", k=k)

        ctoks = []
        for o, i in ((ident[:], ident_d), (maskT[:], maskT_d), (cw[:], cw_d), (cvec[:], cvec_d), (hm[:], hmask),
                     (sgT[:], sgT_d), (bbias, sgbias_d.partition_broadcast(128).rearrange("p o (h t) -> p (o h) t", h=8)), (sgw32, sgw_d.rearrange("h t s -> t h s"))):
            ctoks.append(dma("sp", o, i, "cst"))
        ctok = ctoks[-1]
        rC.did_write(ctok)
        rGB.did_write(ctok)
        t1 = op("dve", lambda e: e.memset(ones32[:], 1.0), writes=[rC2])
        t2 = op("dve", lambda e: e.memset(epsln[:, 0:1], LN_EPS / (ALPHA * ALPHA)), writes=[rC2])
        t3 = op("dve", lambda e: e.memset(epsln[:, 1:2], LN_EPS), writes=[rC2])
        t4 = op("dve", lambda e: e.memset(ones_bf[:], 1.0), writes=[rC2])

        xtoks = []
        for t in range(8):
            xtoks.append(dma("sp", R[t][:], x_main[t * 128:(t + 1) * 128, :], "xld"))
        xtoks.append(dma("sp", HX[:], x_halo, "xld"))
        for t in range(8):
            rR[t].did_write(xtoks[-1])
        rHX.did_write(xtoks[-1])

        for h in range(8):
            b = genbank()
            deps = pbres[b].wr() + rC.rd() + rGB.rd()
            tk = P.emit("pe", lambda e, b=b, h=h: e.transpose(PB[b][:, 0:128], sgw32[:, h, :], ident[:]), deps, sem="pe")
            pbres[b].did_write(tk)
            rGB.did_read(tk)
            op("dve", lambda e, b=b, h=h: e.tensor_tensor(out=WmT[:, h, :], in0=PB[b][:, 0:128], in1=maskT[:], op=ALU.mult),
               reads=[pbres[b], rC], writes=[rWm])
            b2 = genbank()
            mmgroup(b2, [(PB[b2][:, 0:128], ones_bf[:, :], WmT[:, h, :], True, True)], reads=[rWm, rC2])
            tk2 = op("dve", lambda e, b2=b2, h=h: e.scalar_tensor_tensor(out=BB2[:, h, :], in0=PB[b2][:, 0:128], scalar=sgT[:, 8 + h:9 + h],
                                                                   in1=bbias[:, h, :], op0=ALU.mult, op1=ALU.add),
                     reads=[pbres[b2], rC, rGB], writes=[rWm])

        def rtile(t):
            if t == 0:
                return HX, rHX
            return R[t - 1], rR[t - 1]

        def transpose_tile(t):
            src, rs = rtile(t)
            for fb in range(4):
                b = genbank()
                deps = pbres[b].wr() + rs.rd() + rC.rd()
                tk = None
                for i in range(4):
                    f = fb * 4 + i
                    tk = P.emit("pe", lambda e, b=b, i=i, f=f, src=src: e.transpose(
                        PB[b][:, i * 128:(i + 1) * 128], src[:, f * 128:(f + 1) * 128], ident[:]),
                        deps if i == 0 else (), sem="pe" if i == 3 else None)
                pbres[b].did_write(tk)
                rs.did_read(tk)
                eng = "act" if fb % 2 == 0 else "dve"
                o = XT[:, fb * 4:fb * 4 + 4, t * 128:(t + 1) * 128]
                i_ = PB[b][:, :].rearrange("p (f n) -> p f n", f=4)
                if eng == "act":
                    op("act", lambda e, o=o, i_=i_: e.activation(out=o, in_=i_, func=AF.Copy), reads=[pbres[b]], writes=[rXT[t]])
                else:
                    op("dve", lambda e, o=o, i_=i_: e.tensor_copy(out=o, in_=i_), reads=[pbres[b]], writes=[rXT[t]])

        def xt_res(c0, c1):
            return [rXT[t] for t in range(c0 // 128, (c1 + 127) // 128)]

        def pair_block(slot, lA, lB, ncols, func, out_fn, out_res, ws_idx):
            for (c0, c1) in ncols:
                ba, bb = pairbanks()
                n = c1 - c0
                xr = xt_res(c0, c1)
                mmgroup(ba, [(PB[ba][:, 0:n], lA(k), XT[:, k, c0:c1], k == 0, k == KC - 1) for k in range(KC)],
                        reads=xr + [rWS[ws_idx]])
                mmgroup(bb, [(PB[bb][:, 0:n], lB(k), XT[:, k, c0:c1], k == 0, k == KC - 1) for k in range(KC)],
                        reads=xr + [rWS[ws_idx]])
                ti = pair_i[0] % 2
                tmp = TMPr[:, ti * 512:ti * 512 + n]
                op("act", lambda e, tmp=tmp, ba=ba, n=n: e.activation(out=tmp, in_=PB[ba][:, 0:n], func=func),
                   reads=[pbres[ba]], writes=[rTMP[ti]])
                op("dve", lambda e, tmp=tmp, bb=bb, n=n, o=out_fn(c0, c1): e.tensor_tensor(out=o, in0=tmp, in1=PB[bb][:, 0:n], op=ALU.mult),
                   reads=[pbres[bb], rTMP[ti]], writes=[out_res])

        def layer_norm(tiles, li, final=False):
            inherit(rGB, rTMP[0], rTMP[1])
            t_a = dma("sp", g_bc, ln_g[li].partition_broadcast(128), "gb", writes=[rGB])
            t_b = dma("sp", b_bc, ln_b[li].partition_broadcast(128), "gb", writes=[rGB])
            rGB.did_write(t_b)
            for t in tiles:
                src, rs = rtile(t)
                sc = (t % 4) * 16
                for i in range(4):
                    op("dve", lambda e, i=i, src=src: e.bn_stats(out=st[:, i * 6:(i + 1) * 6], in_=src[:, i * 512:(i + 1) * 512]),
                       reads=[rs], writes=[rST])
                op("dve", lambda e: e.bn_aggr(out=st[:, 24:26], in_=st[:, 0:24]), reads=[rST], writes=[rST])
                op("act", lambda e: e.activation(out=st[:, 26:27], in_=st[:, 25:26], func=AF.Sqrt, bias=epsln[:, 0:1], scale=1.0),
                   reads=[rST, rC2], writes=[rST])
                op("dve", lambda e: e.reciprocal(out=st[:, 27:28], in_=st[:, 26:27]), reads=[rST], writes=[rST])
                op("dve", lambda e, src=src: e.scalar_tensor_tensor(out=src[:, :], in0=src[:, :], scalar=st[:, 24:25], in1=g_bc,
                                                                     op0=ALU.subtract, op1=ALU.mult),
                   reads=[rST, rGB], writes=[rs])
                op("dve", lambda e, src=src: e.scalar_tensor_tensor(out=src[:, :], in0=src[:, :], scalar=st[:, 27:28], in1=b_bc,
                                                                     op0=ALU.mult, op1=ALU.add),
                   reads=[rST, rGB], writes=[rs])
                if not final:
                    transpose_tile(t)
            for r_ in rTMP:
                inherit(r_, rGB)

        def ffn(l, tiles, li):
            c_lo = tiles[0] * 128
            c_hi = (tiles[-1] + 1) * 128
            ncols_tot = c_hi - c_lo
            nch = 3 if ncols_tot == NT else 2
            step = ncols_tot // nch
            ncols = [(c_lo + i * step, c_lo + (i + 1) * step) for i in range(nch)]
            wgu = w_gu[l].rearrange("(kc p) n -> p kc n", p=128)
            wdn = w_dn[l].rearrange("(j p) n -> p j n", p=128)
            coef = 0.5 / ALPHA
            for q in range(NQ):
                for jj in range(JQ):
                    j = q * JQ + jj
                    s = wload([(lambda sl: wview(sl, KC, 128, 0), wgu[:, :, j * 128:(j + 1) * 128]),
                               (lambda sl: wview(sl, KC, 128, 2048), wgu[:, :, DFF + j * 128:DFF + (j + 1) * 128])])
                    vg = wview(WS[s], KC, 128, 0)
                    vu = wview(WS[s], KC, 128, 2048)
                    pair_block(WS[s], lambda k, vg=vg: vg[:, k, :], lambda k, vu=vu: vu[:, k, :], ncols, AF.Silu,
                               lambda c0, c1, jj=jj: hT[:, jj, c0:c1], rHT, s)
                for nb in range(8):
                    s = wload([(lambda sl: wview(sl, JQ, 256, 0), wdn[:, q * JQ:(q + 1) * JQ, nb * 256:(nb + 1) * 256])])
                    vd = wview(WS[s], JQ, 256, 0)
                    for t in tiles:
                        b = genbank()
                        mmgroup(b, [(PB[b][:, 0:256], hT[:, jj, t * 128:(t + 1) * 128], vd[:, jj, :], jj == 0, jj == JQ - 1)
                                    for jj in range(JQ)], reads=[rHT, rWS[s]])
                        src, rs = rtile(t)
                        o = src[:, nb * 256:(nb + 1) * 256]
                        op("dve", lambda e, b=b, o=o: e.scalar_tensor_tensor(out=o, in0=PB[b][:, 0:256], scalar=coef, in1=o,
                                                                              op0=ALU.mult, op1=ALU.add),
                           reads=[pbres[b]], writes=[rs])
            return

        for t in range(9):
            transpose_tile(t)

        ffn(0, list(range(9)), 0)
        layer_norm(list(range(9)), 0)


        rBank = Res()
        for r_ in rUT + rV32:
            inherit(r_, rHX)
        for r_ in rVB + rTMP:
            inherit(r_, rGB)
        inherit(rCO, rGB, rHT)
        inherit(rYT, rHT)
        win = w_in.rearrange("(kc p) n -> p kc n", p=128)
        wout = w_out.rearrange("(kc p) n -> p kc n", p=128)
        inv_a = 1.0 / ALPHA

        def wparts(spec):
            kind, i = spec
            if kind == "a":
                return [(lambda sl: wview(sl, KC, 128, 0), win[:, :, 1024 + i * 128:1024 + (i + 1) * 128]),
                        (lambda sl: wview(sl, KC, 128, 2048), win[:, :, i * 128:(i + 1) * 128])]
            if kind == "bv":
                return [(lambda sl: wview(sl, KC, 256, 0), win[:, :, 3072 + i * 256:3072 + (i + 1) * 256])]
            if kind == "bu":
                return [(lambda sl: wview(sl, KC, 256, 0), win[:, :, 2048 + i * 256:2048 + (i + 1) * 256])]
            return [(lambda sl: wview(sl, KC, 256, 0), wout[:, :, i * 256:(i + 1) * 256])]

        specs = []
        for g in range(2):
            specs += [("a", c) for c in range(8)]
            for hp in range(4):
                specs += [("bv", hp), ("bu", hp)]
            specs += [("out", nb) for nb in range(8)]
        wslots = {}
        wn = [0]
        wi = [0]

        def wnext(look=2):
            i = wi[0]
            wi[0] += 1
            while wn[0] <= min(i + look, len(specs) - 1):
                wslots[wn[0]] = wload(wparts(specs[wn[0]]))
                wn[0] += 1
            return wslots[i]

        def tap_ap(k):
            return bank[:, k, :]

        def bankgen(c):
            op("pool", lambda e, c=c: e.tensor_tensor(
                out=bank.bitcast(F32R), in0=ident[:, :].unsqueeze(1).broadcast_to([128, 31, 128]),
                in1=cw[:, c * 31:c * 31 + 31].unsqueeze(2).broadcast_to([128, 31, 128]), op=ALU.mult),
               reads=[rC], writes=[rBank])

        def conv_mm(c):
            hi = c % 2
            hcr = HC[hi].bitcast(F32R)
            b = genbank()
            mmgroup(b, [(PB[b][:, 0:512], tap_ap(k).bitcast(F32R), hcr[:, 98 + k:610 + k], k == 0, k == 30) for k in range(31)],
                    reads=[rHC[hi], rBank])
            op("act", lambda e, b=b, c=c: e.activation(out=COv(c), in_=PB[b][:, 0:512], func=AF.Identity,
                                                       bias=cvec[:, c:c + 1], scale=1.0),
               reads=[pbres[b], rC], writes=[rCO])

        for g in range(2):
            gc0 = g * 512
            pending = None
            for c in range(8):
                s = wnext()
                vgt = wview(WS[s], KC, 128, 0)
                vvl = wview(WS[s], KC, 128, 2048)
                hi = c % 2
                hcr = HC[hi].bitcast(F32R)
                pair_block(WS[s], lambda k, v=vgt: v[:, k, :], lambda k, v=vvl: v[:, k, :],
                           [(gc0, gc0 + 320), (gc0 + 320, gc0 + 640)], AF.Sigmoid,
                           lambda c0, c1, hcr=hcr, gc0=gc0: hcr[:, c0 - gc0:c1 - gc0], rHC[hi], s)
                if g == 0:
                    op("dve", lambda e, hi=hi, hcr=hcr: e.tensor_scalar(out=hcr[:, 0:128], in0=HC[hi][:, 0:128], scalar1=hm[:, 0:1],
                                                                         scalar2=None, op0=ALU.mult), reads=[rC], writes=[rHC[hi]])
                if pending is not None:
                    pending()
                bankgen(c)
                pending = (lambda c=c: conv_mm(c))
            pending()
            for hp in range(4):
                s = wnext()
                vbv = wview(WS[s], KC, 256, 0)
                vi = hp % 2
                for tg in range(4):
                    tcol = 128 + (g * 4 + tg) * 128
                    b = genbank()
                    mmgroup(b, [(PB[b][:, 0:256], XT[:, k, tcol:tcol + 128], vbv[:, k, :], k == 0, k == KC - 1) for k in range(KC)],
                            reads=xt_res(tcol, tcol + 128) + [rWS[s]])
                    for hh in range(2):
                        sl = slice(hh * 128, (hh + 1) * 128)
                        op("act", lambda e, b=b, tg=tg, hh=hh, sl=sl: e.activation(out=V32[tg][:, sl], in_=PB[b][:, sl], func=AF.Gelu_apprx_tanh,
                                                                                 accum_out=st[:, 32 + tg * 2 + hh:33 + tg * 2 + hh]),
                           reads=[pbres[b]], writes=[rV32[tg], rST])
                        op("act", lambda e, tg=tg, hh=hh, sl=sl: e.activation(out=junk[:, :], in_=V32[tg][:, sl], func=AF.Square,
                                                                            accum_out=st[:, 40 + tg * 2 + hh:41 + tg * 2 + hh]),
                           reads=[rV32[tg]], writes=[rST, rJ])
                op("dve", lambda e: e.tensor_scalar(out=st[:, 48:56], in0=st[:, 32:40], scalar1=1.0 / 128, scalar2=None, op0=ALU.mult),
                   reads=[rST], writes=[rST])
                op("dve", lambda e: e.tensor_tensor(out=st[:, 56:64], in0=st[:, 48:56], in1=st[:, 48:56], op=ALU.mult),
                   reads=[rST], writes=[rST])
                op("dve", lambda e: e.scalar_tensor_tensor(out=st[:, 64:72], in0=st[:, 40:48], scalar=1.0 / 128, in1=st[:, 56:64],
                                                           op0=ALU.mult, op1=ALU.subtract), reads=[rST], writes=[rST])
                op("act", lambda e: e.activation(out=st[:, 72:80], in_=st[:, 64:72], func=AF.Sqrt, bias=epsln[:, 1:2], scale=1.0),
                   reads=[rST, rC2], writes=[rST])
                op("dve", lambda e: e.reciprocal(out=st[:, 80:88], in_=st[:, 72:80]), reads=[rST], writes=[rST])
                op("dve", lambda e: e.scalar_tensor_tensor(out=st[:, 88:96], in0=st[:, 48:56], scalar=-1.0, in1=st[:, 80:88],
                                                           op0=ALU.mult, op1=ALU.mult), reads=[rST], writes=[rST])
                for tg in range(4):
                    for hh in range(2):
                        sl = slice(hh * 128, (hh + 1) * 128)
                        col = tg * 2 + hh
                        op("act", lambda e, tg=tg, sl=sl, col=col, vi=vi: e.activation(out=VB[vi][:, tg, sl], in_=V32[tg][:, sl], func=AF.Identity,
                                                                                     bias=st[:, 88 + col:89 + col], scale=st[:, 80 + col:81 + col]),
                           reads=[rST, rV32[tg]], writes=[rVB[vi]])
                s2 = wnext()
                vbu = wview(WS[s2], KC, 256, 0)
                for hh in range(2):
                    h = hp * 2 + hh
                    b = genbank()
                    mc0 = 128 + g * 512
                    mmgroup(b, [(PB[b][:, 0:512], vbu[:, k, hh * 128:(hh + 1) * 128], XT[:, k, mc0:mc0 + 512], k == 0, k == KC - 1)
                                for k in range(KC)], reads=xt_res(mc0, mc0 + 512) + [rWS[s2]])
                    ui = h % 2
                    op("act", lambda e, b=b, ui=ui: e.activation(out=UT[ui], in_=PB[b][:, 0:512], func=AF.Gelu_apprx_tanh),
                       reads=[pbres[b]], writes=[rUT[ui]])
                    b2 = genbank()
                    mmgroup(b2, [(PB[b2][:, tg * 128:(tg + 1) * 128], VB[vi][:, tg, hh * 128:(hh + 1) * 128], WmT[:, h, :], True, True)
                                 for tg in range(4)], reads=[rVB[vi], rWm])
                    tmp = TMPr[:, ui * 512:(ui + 1) * 512]
                    op("dve", lambda e, b2=b2, h=h, tmp=tmp: e.scalar_tensor_tensor(
                        out=tmp.rearrange("p (c t) -> p c t", c=4), in0=PB[b2][:, :].rearrange("p (c t) -> p c t", c=4),
                        scalar=sgT[:, h:h + 1], in1=BB2[:, h, :].unsqueeze(1).broadcast_to([128, 4, 128]), op0=ALU.mult, op1=ALU.add),
                       reads=[pbres[b2], rC, rWm], writes=[rTMP[ui]])
                    op("dve", lambda e, ui=ui, h=h, tmp=tmp: e.tensor_tensor(out=YT[:, 8 + h, :], in0=tmp, in1=UT[ui], op=ALU.mult),
                       reads=[rTMP[ui], rUT[ui]], writes=[rYT])
            bS = genbank()
            mmgroup(bS, [(PB[bS][:, :], ones32[:, :], COv(c), c == 0, c == 7) for c in range(8)], reads=[rCO, rC2])
            bQ = genbank()
            for c in range(8):
                qi = c % 2
                sq = TMPr[:, qi * 512:(qi + 1) * 512]
                op("act", lambda e, sq=sq, c=c: e.activation(out=sq, in_=COv(c), func=AF.Square), reads=[rCO], writes=[rTMP[qi]])
                mmgroup(bQ, [(PB[bQ][:, :], ones32[:, :], sq, c == 0, c == 7)], reads=[rTMP[qi], rC2])
            T0 = TMPr[:, 0:512]
            T1 = TMPr[:, 512:1024]
            op("act", lambda e, bS=bS: e.activation(out=PB[bS][:, :], in_=PB[bS][:, :], func=AF.Copy, scale=1.0 / 1024),
               writes=[pbres[bS]])
            op("act", lambda e, bS=bS: e.activation(out=T0, in_=PB[bS][:, :], func=AF.Square), reads=[pbres[bS]], writes=[rTMP[0]])
            op("dve", lambda e, bQ=bQ: e.scalar_tensor_tensor(out=PB[bQ][:, :], in0=PB[bQ][:, :], scalar=1.0 / 1024, in1=T0,
                                                              op0=ALU.mult, op1=ALU.subtract), reads=[rTMP[0]], writes=[pbres[bQ]])
            op("act", lambda e, bQ=bQ: e.activation(out=PB[bQ][:, :], in_=PB[bQ][:, :], func=AF.Sqrt, bias=epsln[:, 1:2], scale=1.0),
               reads=[rC2], writes=[pbres[bQ]])
            op("dve", lambda e, bQ=bQ: e.reciprocal(out=T1, in_=PB[bQ][:, :]), reads=[pbres[bQ]], writes=[rTMP[1]])
            op("dve", lambda e, bS=bS: e.scalar_tensor_tensor(out=PB[bS][:, :], in0=PB[bS][:, :], scalar=-1.0, in1=T1,
                                                              op0=ALU.mult, op1=ALU.mult), reads=[rTMP[1]], writes=[pbres[bS]])
            for c in range(8):
                op("dve", lambda e, c=c: e.tensor_tensor(out=COv(c), in0=COv(c), in1=T1, op=ALU.mult),
                   reads=[rTMP[1]], writes=[rCO])
                op("dve", lambda e, c=c, bS=bS: e.tensor_tensor(out=COv(c), in0=COv(c), in1=PB[bS][:, :], op=ALU.add),
                   reads=[pbres[bS]], writes=[rCO])
                op("act", lambda e, c=c: e.activation(out=YT[:, c, :], in_=COv(c), func=AF.Silu,
                                                      bias=cvec[:, 16 + c:17 + c], scale=cvec[:, 8 + c:9 + c]),
                   reads=[rCO, rC], writes=[rYT])
            for nb in range(8):
                s = wnext()
                vo = wview(WS[s], KC, 256, 0)
                for tg in range(4):
                    t = 1 + g * 4 + tg
                    b = genbank()
                    mmgroup(b, [(PB[b][:, 0:256], YT[:, kk, tg * 128:(tg + 1) * 128], vo[:, kk, :], kk == 0, kk == KC - 1)
                                for kk in range(KC)], reads=[rYT, rWS[s]])
                    src, rs = rtile(t)
                    o = src[:, nb * 256:(nb + 1) * 256]
                    op("dve", lambda e, b=b, o=o: e.scalar_tensor_tensor(out=o, in0=PB[b][:, 0:256], scalar=inv_a, in1=o,
                                                                          op0=ALU.mult, op1=ALU.add),
                       reads=[pbres[b]], writes=[rs])
        for r_ in rVB + rTMP + [rCO]:
            inherit(rGB, r_)
        inherit(rHT, rYT, rCO)

        layer_norm(list(range(1, 9)), 1)
        ffn(1, list(range(1, 9)), 2)
        layer_norm(list(range(1, 9)), 2, final=True)

        otok = None
        for t in range(8):
            otok = dma("sp", out_d[t * 128:(t + 1) * 128, :], R[t][:], "ost", reads=[rR[t]])
        P.emit("sp", lambda e: e.nop(), [otok])

        P.build(nc, sems)
    return nc


_NC_CACHE = {}


def kernel(x, ffn1_w_gate_up, ffn1_w_down, ln1_g, ln1_b, mix_w_in, conv_w, conv_b, conv_ln_g, conv_ln_b,
           sg_ln_g, sg_ln_b, sg_w, sg_b, mix_w_out, ln2_g, ln2_b, ffn2_w_gate_up, ffn2_w_down, ln3_g, ln3_b):
    f = lambda a: np.ascontiguousarray(np.asarray(a, dtype=np.float32))
    x = f(x)
    common = {
        "w_gu1": f(ffn1_w_gate_up)[0], "w_gu2": f(ffn2_w_gate_up)[0],
        "w_dn1": f(ffn1_w_down)[0], "w_dn2": f(ffn2_w_down)[0],
        "w_in": f(mix_w_in)[0], "w_out": f(mix_w_out)[0],
        "ln1_g": f(ln1_g).reshape(1, D), "ln1_b": f(ln1_b).reshape(1, D),
        "ln2_g": f(ln2_g).reshape(1, D), "ln2_b": f(ln2_b).reshape(1, D),
        "ln3_g": f(ln3_g).reshape(1, D), "ln3_b": f(ln3_b).reshape(1, D),
        "cw": np.ascontiguousarray(f(conv_w)[0].reshape(31, 8, 128).transpose(2, 1, 0).reshape(128, 8 * 31)),
        "cvec": np.ascontiguousarray(np.concatenate([f(conv_b)[0].reshape(8, 128).T, f(conv_ln_g)[0].reshape(8, 128).T,
                                                     f(conv_ln_b)[0].reshape(8, 128).T], axis=1)),
        "sgT": np.ascontiguousarray(np.concatenate([f(sg_ln_g).reshape(8, 128).T, f(sg_ln_b).reshape(8, 128).T], axis=1)),
        "sgw": f(sg_w)[0], "sgbias": f(sg_b).reshape(1, 1024),
        "ident": np.eye(128, dtype=np.float32),
        "maskT": np.triu(np.ones((128, 128), dtype=np.float32)),
    }
    in_maps = []
    for c in range(NCORES):
        b, s0 = c // 4, (c % 4) * TOK
        m = dict(common)
        m["x_main"] = np.ascontiguousarray(x[b, s0:s0 + TOK])
        if s0 > 0:
            m["x_halo"] = np.ascontiguousarray(x[b, s0 - HALO:s0])
            m["hmask"] = np.ones((128, 1), np.float32)
        else:
            m["x_halo"] = np.zeros((HALO, D), np.float32)
            m["hmask"] = np.zeros((128, 1), np.float32)
        in_maps.append(m)
    if "nc" not in _NC_CACHE:
        _NC_CACHE["nc"] = build_program()
    nc = _NC_CACHE["nc"]
    res = run_bass_kernel_spmd(nc, in_maps, core_ids=list(range(NCORES)))
    out = np.empty((2, 4096, D), np.float32)
    for c in range(NCORES):
        b, s0 = c // 4, (c % 4) * TOK
        out[b, s0:s0 + TOK] = res.results[c]["out"]
    return out
```

```python
from contextlib import ExitStack
import numpy as np
import concourse.bass as bass
import concourse.mybir as mybir
from concourse.bass_utils import run_bass_kernel_spmd

F32 = mybir.dt.float32
F32R = mybir.dt.float32r
BF16 = mybir.dt.bfloat16
AF = mybir.ActivationFunctionType
ALU = mybir.AluOpType
AX = mybir.AxisListType

NCORES = 8
D = 2048
DFF = 5632
TOK = 1024
HALO = 128
NT = TOK + HALO
KC = D // 128
NQ = 4
JQ = DFF // 128 // NQ
ALPHA = 2.0 ** 0.25
LN_EPS = 1e-5
ENGS = ["pe", "act", "dve", "pool", "sp"]
ESEM = {"pe": "pe", "act": "act", "dve": "dve", "pool": "pool"}


class Prog:
    def __init__(self):
        self.ops = {e: [] for e in ENGS}
        self.cnt = {}
        self.waited = {e: {} for e in ENGS}

    def newsem(self, name):
        self.cnt[name] = 0
        return name

    def emit(self, eng, fn, deps=(), sem=None, inc=1):
        waits = []
        for d in deps:
            if d is None:
                continue
            s, v = d
            if self.waited[eng].get(s, 0) >= v:
                continue
            self.waited[eng][s] = v
            waits.append((s, v))
        tok = None
        if sem is not None:
            self.cnt[sem] += inc
            tok = (sem, self.cnt[sem])
        self.ops[eng].append((waits, fn, sem, inc))
        return tok

    def build(self, nc, sems):
        handles = {"pe": "tensor", "act": "scalar", "dve": "vector", "pool": "gpsimd", "sp": "sync"}
        with nc.Block() as block:
            for e in ENGS:
                ops = self.ops[e]

                def body(engine, ops=ops):
                    for waits, fn, sem, inc in ops:
                        for s, v in waits:
                            engine.wait_ge(sems[s], v)
                        ins = fn(engine)
                        if sem is not None:
                            ins.then_inc(sems[sem], inc)

                getattr(block, handles[e])(body)


class Res:
    def __init__(self):
        self.w = None
        self.r = {}

    def rd(self):
        return [self.w]

    def wr(self):
        return [self.w] + [(s, v) for s, v in self.r.items()]

    def did_read(self, t):
        if t is not None:
            s, v = t
            self.r[s] = max(self.r.get(s, 0), v)

    def did_write(self, t):
        self.w = t
        self.r = {}


def build_program(debug=False):
    nc = bass.Bass("TRN2", target_bir_lowering=False)
    P = Prog()

    def din(name, shape):
        return nc.dram_tensor(name, list(shape), F32, kind="ExternalInput").ap()

    x_main = din("x_main", [TOK, D])
    x_halo = din("x_halo", [HALO, D])
    hmask = din("hmask", [128, 1])
    w_gu = [din("w_gu1", [D, 2 * DFF]), din("w_gu2", [D, 2 * DFF])]
    w_dn = [din("w_dn1", [DFF, D]), din("w_dn2", [DFF, D])]
    w_in = din("w_in", [D, 4096])
    w_out = din("w_out", [D, D])
    ln_g = [din(f"ln{i}_g", [1, D]) for i in (1, 2, 3)]
    ln_b = [din(f"ln{i}_b", [1, D]) for i in (1, 2, 3)]
    cw_d = din("cw", [128, 8 * 31])
    cvec_d = din("cvec", [128, 24])
    sgT_d = din("sgT", [128, 16])
    sgw_d = din("sgw", [8, 128, 128])
    sgbias_d = din("sgbias", [1, 1024])
    ident_d = din("ident", [128, 128])
    maskT_d = din("maskT", [128, 128])
    out_d = nc.dram_tensor("out", [TOK, D], F32, kind="ExternalOutput").ap()

    with ExitStack() as es:
        def sb(name, shape, dt):
            return es.enter_context(nc.sbuf_tensor("s_" + name, shape, dt))

        def ps(name):
            return es.enter_context(nc.psum_tensor(name, [128, 512], F32))

        R = [sb(f"R{t}", [128, D], F32) for t in range(8)]
        HX = sb("HX", [128, D], F32)
        XT = sb("XT", [128, KC, NT], BF16)
        HTr = sb("HTr", [128, JQ * NT], BF16)
        GBr = sb("GBr", [128, 2 * D], F32)
        HCr = sb("HCr", [128, 1280], F32)
        BANK = sb("BANK", [128, 31 * 128], F32)
        WS = [sb(f"WS{i}", [128, 4096], BF16) for i in range(3)]
        ident = sb("ident", [128, 128], F32)
        maskT = sb("maskT", [128, 128], F32)
        ones32 = sb("ones32", [128, 128], F32)
        WmT = sb("WmT", [128, 8, 128], BF16)
        BB2 = sb("BB2", [128, 8, 128], F32)
        sgT = sb("sgT", [128, 16], F32)
        ones_bf = sb("ones_bf", [128, 128], BF16)
        junk = sb("junk", [128, 128], F32)
        cw = sb("cw", [128, 8 * 31], F32)
        cvec = sb("cvec", [128, 24], F32)
        hm = sb("hm", [128, 1], F32)
        st = sb("st", [128, 128], F32)
        STT = sb("STT", [128, 9, 48], F32)
        epsln = sb("epsln", [128, 2], F32)

        hT = HTr[:, :].rearrange("p (j t) -> p j t", j=JQ)
        YT = HTr[:, 0:16 * 512].rearrange("p (k t) -> p k t", k=16)
        g_bc = GBr[:, 0:D]
        b_bc = GBr[:, D:2 * D]
        TMPr = GBr[:, 3072:4096]
        VBr = GBr[:, 2048:3072].bitcast(BF16)
        VB = [VBr[:, i * 1024:(i + 1) * 1024].rearrange("p (t d) -> p t d", t=4) for i in range(2)]
        UT = [HX[:, i * 512:(i + 1) * 512] for i in range(2)]
        V32 = [HX[:, 1024 + i * 256:1024 + (i + 1) * 256] for i in range(4)]
        HC = [HCr[:, i * 640:(i + 1) * 640] for i in range(2)]
        bank = BANK[:, :].rearrange("p (k n) -> p k n", k=31)
        sgw32 = GBr[:, 0:1024].rearrange("p (h s) -> p h s", h=8)
        bbias = GBr[:, 1024:2048].rearrange("p (h t) -> p h t", h=8)

        def COv(c):
            if c < 4:
                return GBr[:, c * 512:(c + 1) * 512]
            return HTr[:, 8192 + (c - 4) * 1024:8192 + (c - 3) * 1024].bitcast(F32)

        PB = [ps(f"pb{i}") for i in range(8)]
        pbres = [Res() for _ in range(8)]

        semnames = ["pe", "act", "dve", "pool", "xld", "cst", "gb", "ost", "sgl", "w0", "w1", "w2"]
        for n in semnames:
            P.newsem(n)
        sems = {n: es.enter_context(nc.semaphore(n)) for n in semnames}

        rR = [Res() for _ in range(8)]
        rHX = Res()
        rXT = [Res() for _ in range(9)]
        rHT = Res()
        rGB = Res()
        rTMP = [Res(), Res()]
        rCO = Res()
        rWS = [Res() for _ in range(3)]
        rC = Res()
        rC2 = Res()
        rWm = Res()
        rST = Res()
        rSTT = [Res() for _ in range(9)]
        rLS = [Res(), Res()]
        rUT = [Res(), Res()]
        rV32 = [Res() for _ in range(4)]
        rJ = Res()
        rSQ = [Res(), Res()]
        rVB = [Res(), Res()]
        rHC = [Res(), Res()]
        rYT = Res()

        def inherit(dst, *owners):
            for o in owners:
                for tk in o.wr():
                    dst.did_read(tk)

        def op(eng, fn, reads=(), writes=(), extra=()):
            deps = list(extra)
            for r in reads:
                deps += r.rd()
            for w in writes:
                deps += w.wr()
            tok = P.emit(eng, fn, deps, sem=ESEM[eng])
            for r in reads:
                r.did_read(tok)
            for w in writes:
                w.did_write(tok)
            return tok

        def dma(eng, out_ap, in_ap, sem, reads=(), writes=()):
            deps = []
            for r in reads:
                deps += r.rd()
            for w in writes:
                deps += w.wr()
            return P.emit(eng, lambda e: e.dma_start(out=out_ap, in_=in_ap), deps, sem=sem, inc=16)

        def mmgroup(bank, mms, reads):
            deps = pbres[bank].wr()
            for r in reads:
                deps += r.rd()
            tok = None
            n = len(mms)
            for i, (o, l, r_, st_, sp_) in enumerate(mms):
                tok = P.emit("pe", lambda e, o=o, l=l, r_=r_, st_=st_, sp_=sp_: e.matmul(o, l, r_, start=st_, stop=sp_),
                             deps if i == 0 else (), sem="pe" if i == n - 1 else None)
            pbres[bank].did_write(tok)
            for r in reads:
                r.did_read(tok)
            return tok

        gen_i = [0]

        def genbank():
            b = 4 + gen_i[0] % 4
            gen_i[0] += 1
            return b

        pair_i = [0]

        def pairbanks():
            s = pair_i[0] % 2
            pair_i[0] += 1
            return 2 * s, 2 * s + 1

        ws_i = [0]

        def wload(parts):
            s = ws_i[0] % 3
            ws_i[0] += 1
            deps = rWS[s].wr()
            tok = None
            for vf, src in parts:
                tok = P.emit("pool", lambda e, o=vf(WS[s]), i=src: e.dma_start(out=o, in_=i), deps, sem=f"w{s}", inc=16)
            rWS[s].did_write(tok)
            return s

        def wview(slot, k, n, off=0):
            return slot[:, off:off + k * n].rearrange("p (k n) -> p k n", k=k)

        wgu_v = [w.rearrange("(kc p) n -> p kc n", p=128) for w in w_gu]
        wdn_v = [w.rearrange("(j p) n -> p j n", p=128) for w in w_dn]
        win = w_in.rearrange("(kc p) n -> p kc n", p=128)
        wout = w_out.rearrange("(kc p) n -> p kc n", p=128)

        def wparts(spec):
            kind = spec[0]
            if kind == "gu":
                _, l, j = spec
                return [(lambda sl: wview(sl, KC, 128, 0), wgu_v[l][:, :, j * 128:(j + 1) * 128]),
                        (lambda sl: wview(sl, KC, 128, 2048), wgu_v[l][:, :, DFF + j * 128:DFF + (j + 1) * 128])]
            if kind == "dn":
                _, l, q, nb = spec
                return [(lambda sl: wview(sl, JQ, 256, 0), wdn_v[l][:, q * JQ:(q + 1) * JQ, nb * 256:(nb + 1) * 256])]
            i = spec[1]
            if kind == "a":
                return [(lambda sl: wview(sl, KC, 128, 0), win[:, :, 1024 + i * 128:1024 + (i + 1) * 128]),
                        (lambda sl: wview(sl, KC, 128, 2048), win[:, :, i * 128:(i + 1) * 128])]
            if kind == "bv":
                return [(lambda sl: wview(sl, KC, 256, 0), win[:, :, 3072 + i * 256:3072 + (i + 1) * 256])]
            if kind == "bu":
                return [(lambda sl: wview(sl, KC, 256, 0), win[:, :, 2048 + i * 256:2048 + (i + 1) * 256])]
            return [(lambda sl: wview(sl, KC, 256, 0), wout[:, :, i * 256:(i + 1) * 256])]

        specs = []

        def ffn_specs(l):
            for q in range(NQ):
                specs.extend(("gu", l, q * JQ + jj) for jj in range(JQ))
                specs.extend(("dn", l, q, nb) for nb in range(8))

        ffn_specs(0)
        for g in range(2):
            specs.extend(("a", c, g) for c in range(8))
            for hp in range(4):
                specs.extend([("bv", hp, g), ("bu", hp, g)])
            specs.extend(("out", nb, g) for nb in range(8))
        ffn_specs(1)
        wslots = {}
        wn = [0]
        wi = [0]

        def wprefetch(look=2):
            while wn[0] <= min(wi[0] + look, len(specs) - 1):
                wslots[wn[0]] = wload(wparts(specs[wn[0]]))
                wn[0] += 1

        def wnext(kind, look=2):
            i = wi[0]
            assert specs[i][0] == kind, (specs[i], kind)
            wprefetch(look)
            wi[0] += 1
            return wslots[i]

        ctoks = []
        for o, i in ((ident[:], ident_d), (maskT[:], maskT_d), (cw[:], cw_d), (cvec[:], cvec_d), (hm[:], hmask),
                     (sgT[:], sgT_d), (bbias, sgbias_d.partition_broadcast(128).rearrange("p o (h t) -> p (o h) t", h=8)), (sgw32, sgw_d.rearrange("h t s -> t h s"))):
            ctoks.append(dma("sp", o, i, "cst"))
        ctok = ctoks[-1]
        rC.did_write(ctok)
        rGB.did_write(ctok)
        t1 = op("dve", lambda e: e.memset(ones32[:], 1.0), writes=[rC2])
        t2 = op("dve", lambda e: e.memset(epsln[:, 0:1], LN_EPS / (ALPHA * ALPHA)), writes=[rC2])
        t3 = op("dve", lambda e: e.memset(epsln[:, 1:2], LN_EPS), writes=[rC2])
        t4 = op("dve", lambda e: e.memset(ones_bf[:], 1.0), writes=[rC2])

        xtoks = []
        for t in range(8):
            xtoks.append(dma("sp", R[t][:], x_main[t * 128:(t + 1) * 128, :], "xld"))
        xtoks.append(dma("sp", HX[:], x_halo, "xld"))
        for t in range(8):
            rR[t].did_write(xtoks[-1])
        rHX.did_write(xtoks[-1])

        for h in range(8):
            b = genbank()
            deps = pbres[b].wr() + rC.rd() + rGB.rd()
            tk = P.emit("pe", lambda e, b=b, h=h: e.transpose(PB[b][:, 0:128], sgw32[:, h, :], ident[:]), deps, sem="pe")
            pbres[b].did_write(tk)
            rGB.did_read(tk)
            op("dve", lambda e, b=b, h=h: e.tensor_tensor(out=WmT[:, h, :], in0=PB[b][:, 0:128], in1=maskT[:], op=ALU.mult),
               reads=[pbres[b], rC], writes=[rWm])
            b2 = genbank()
            mmgroup(b2, [(PB[b2][:, 0:128], ones_bf[:, :], WmT[:, h, :], True, True)], reads=[rWm, rC2])
            tk2 = op("dve", lambda e, b2=b2, h=h: e.scalar_tensor_tensor(out=BB2[:, h, :], in0=PB[b2][:, 0:128], scalar=sgT[:, 8 + h:9 + h],
                                                                   in1=bbias[:, h, :], op0=ALU.mult, op1=ALU.add),
                     reads=[pbres[b2], rC, rGB], writes=[rWm])

        def rtile(t):
            if t == 0:
                return HX, rHX
            return R[t - 1], rR[t - 1]

        def transpose_tile(t):
            src, rs = rtile(t)
            for fb in range(4):
                b = genbank()
                deps = pbres[b].wr() + rs.rd() + rC.rd()
                tk = None
                for i in range(4):
                    f = fb * 4 + i
                    tk = P.emit("pe", lambda e, b=b, i=i, f=f, src=src: e.transpose(
                        PB[b][:, i * 128:(i + 1) * 128], src[:, f * 128:(f + 1) * 128], ident[:]),
                        deps if i == 0 else (), sem="pe" if i == 3 else None)
                pbres[b].did_write(tk)
                rs.did_read(tk)
                eng = "act" if fb % 2 == 0 else "dve"
                o = XT[:, fb * 4:fb * 4 + 4, t * 128:(t + 1) * 128]
                i_ = PB[b][:, :].rearrange("p (f n) -> p f n", f=4)
                if eng == "act":
                    op("act", lambda e, o=o, i_=i_: e.activation(out=o, in_=i_, func=AF.Copy), reads=[pbres[b]], writes=[rXT[t]])
                else:
                    op("dve", lambda e, o=o, i_=i_: e.tensor_copy(out=o, in_=i_), reads=[pbres[b]], writes=[rXT[t]])

        def xt_res(c0, c1):
            return [rXT[t] for t in range(c0 // 128, (c1 + 127) // 128)]

        def pair_block(slot, lA, lB, ncols, func, out_fn, out_res, ws_idx):
            for (c0, c1) in ncols:
                ba, bb = pairbanks()
                n = c1 - c0
                xr = xt_res(c0, c1)
                mmgroup(ba, [(PB[ba][:, 0:n], lA(k), XT[:, k, c0:c1], k == 0, k == KC - 1) for k in range(KC)],
                        reads=xr + [rWS[ws_idx]])
                mmgroup(bb, [(PB[bb][:, 0:n], lB(k), XT[:, k, c0:c1], k == 0, k == KC - 1) for k in range(KC)],
                        reads=xr + [rWS[ws_idx]])
                ti = pair_i[0] % 2
                tmp = TMPr[:, ti * 512:ti * 512 + n]
                op("act", lambda e, tmp=tmp, ba=ba, n=n: e.activation(out=tmp, in_=PB[ba][:, 0:n], func=func),
                   reads=[pbres[ba]], writes=[rTMP[ti]])
                op("dve", lambda e, tmp=tmp, bb=bb, n=n, o=out_fn(c0, c1): e.tensor_tensor(out=o, in0=tmp, in1=PB[bb][:, 0:n], op=ALU.mult),
                   reads=[pbres[bb], rTMP[ti]], writes=[out_res])

        otoks = []

        def out_dma(t):
            otoks.append(dma("sp", out_d[(t - 1) * 128:t * 128, :], R[t - 1][:], "ost", reads=[rR[t - 1]]))

        def layer_norm(tiles, li, final=False):
            inherit(rGB, rTMP[0], rTMP[1])
            t_a = dma("sp", g_bc, ln_g[li].partition_broadcast(128), "gb", writes=[rGB])
            t_b = dma("sp", b_bc, ln_b[li].partition_broadcast(128), "gb", writes=[rGB])
            rGB.did_write(t_b)
            wprefetch()
            for t in tiles:
                src, rs = rtile(t)
                c0 = (t % 2) * 4
                op("dve", lambda e, t=t, c0=c0: e.bn_aggr(out=st[:, c0:c0 + 2], in_=STT[:, t, :]), reads=[rSTT[t]], writes=[rLS[t % 2]])
                op("act", lambda e, c0=c0: e.activation(out=st[:, c0 + 2:c0 + 3], in_=st[:, c0 + 1:c0 + 2], func=AF.Sqrt,
                                                        bias=epsln[:, 0:1], scale=1.0), reads=[rLS[t % 2], rC2], writes=[rLS[t % 2]])
                op("dve", lambda e, c0=c0: e.reciprocal(out=st[:, c0 + 3:c0 + 4], in_=st[:, c0 + 2:c0 + 3]), reads=[rLS[t % 2]], writes=[rLS[t % 2]])
                op("dve", lambda e, src=src, c0=c0: e.scalar_tensor_tensor(out=src[:, :], in0=src[:, :], scalar=st[:, c0:c0 + 1], in1=g_bc,
                                                                            op0=ALU.subtract, op1=ALU.mult),
                   reads=[rLS[t % 2], rGB], writes=[rs])
                op("act", lambda e, src=src, c0=c0: e.activation(out=src[:, :], in_=src[:, :], func=AF.Identity, bias=0.0,
                                                                 scale=st[:, c0 + 3:c0 + 4]), reads=[rLS[t % 2]], writes=[rs])
                op("pool", lambda e, src=src: e.tensor_tensor(out=src[:, :], in0=src[:, :], in1=b_bc, op=ALU.add), reads=[rGB], writes=[rs])
                if not final:
                    transpose_tile(t)
                else:
                    out_dma(t)
            for r_ in rTMP:
                inherit(r_, rGB)

        def ffn(l, tiles, li):
            c_lo = tiles[0] * 128
            c_hi = (tiles[-1] + 1) * 128
            ncols_tot = c_hi - c_lo
            nch = 3 if ncols_tot == NT else 2
            step = ncols_tot // nch
            ncols = [(c_lo + i * step, c_lo + (i + 1) * step) for i in range(nch)]
            coef = 0.5 / ALPHA
            for q in range(NQ):
                for jj in range(JQ):
                    j = q * JQ + jj
                    s = wnext("gu")
                    vg = wview(WS[s], KC, 128, 0)
                    vu = wview(WS[s], KC, 128, 2048)
                    pair_block(WS[s], lambda k, vg=vg: vg[:, k, :], lambda k, vu=vu: vu[:, k, :], ncols, AF.Silu,
                               lambda c0, c1, jj=jj: hT[:, jj, c0:c1], rHT, s)
                for nb in range(8):
                    s = wnext("dn")
                    vd = wview(WS[s], JQ, 256, 0)
                    for t in tiles:
                        b = genbank()
                        mmgroup(b, [(PB[b][:, 0:256], hT[:, jj, t * 128:(t + 1) * 128], vd[:, jj, :], jj == 0, jj == JQ - 1)
                                    for jj in range(JQ)], reads=[rHT, rWS[s]])
                        src, rs = rtile(t)
                        o = src[:, nb * 256:(nb + 1) * 256]
                        op("dve", lambda e, b=b, o=o: e.scalar_tensor_tensor(out=o, in0=PB[b][:, 0:256], scalar=coef, in1=o,
                                                                              op0=ALU.mult, op1=ALU.add),
                           reads=[pbres[b]], writes=[rs])
                        if q == NQ - 1:
                            op("dve", lambda e, o=o, t=t, nb=nb: e.bn_stats(out=STT[:, t, nb * 6:(nb + 1) * 6], in_=o),
                               reads=[rs], writes=[rSTT[t]])
            return

        for t in range(9):
            transpose_tile(t)

        ffn(0, list(range(9)), 0)
        layer_norm(list(range(9)), 0)


        rBank = Res()
        for r_ in rUT + rV32:
            inherit(r_, rHX)
        for r_ in rVB + rTMP:
            inherit(r_, rGB)
        inherit(rCO, rGB, rHT)
        inherit(rYT, rHT)
        inv_a = 1.0 / ALPHA

        def tap_ap(k):
            return bank[:, k, :]

        def bankgen(c):
            op("pool", lambda e, c=c: e.tensor_tensor(
                out=bank.bitcast(F32R), in0=ident[:, :].unsqueeze(1).broadcast_to([128, 31, 128]),
                in1=cw[:, c * 31:c * 31 + 31].unsqueeze(2).broadcast_to([128, 31, 128]), op=ALU.mult),
               reads=[rC], writes=[rBank])

        def conv_mm(c):
            hi = c % 2
            hcr = HC[hi].bitcast(F32R)
            b = genbank()
            mmgroup(b, [(PB[b][:, 0:512], tap_ap(k).bitcast(F32R), hcr[:, 98 + k:610 + k], k == 0, k == 30) for k in range(31)],
                    reads=[rHC[hi], rBank])
            op("act", lambda e, b=b, c=c: e.activation(out=COv(c), in_=PB[b][:, 0:512], func=AF.Identity,
                                                       bias=cvec[:, c:c + 1], scale=1.0),
               reads=[pbres[b], rC], writes=[rCO])

        for g in range(2):
            gc0 = g * 512
            pending = None
            for c in range(8):
                s = wnext("a")
                vgt = wview(WS[s], KC, 128, 0)
                vvl = wview(WS[s], KC, 128, 2048)
                hi = c % 2
                hcr = HC[hi].bitcast(F32R)
                pair_block(WS[s], lambda k, v=vgt: v[:, k, :], lambda k, v=vvl: v[:, k, :],
                           [(gc0, gc0 + 320), (gc0 + 320, gc0 + 640)], AF.Sigmoid,
                           lambda c0, c1, hcr=hcr, gc0=gc0: hcr[:, c0 - gc0:c1 - gc0], rHC[hi], s)
                if g == 0:
                    op("dve", lambda e, hi=hi, hcr=hcr: e.tensor_scalar(out=hcr[:, 0:128], in0=HC[hi][:, 0:128], scalar1=hm[:, 0:1],
                                                                         scalar2=None, op0=ALU.mult), reads=[rC], writes=[rHC[hi]])
                if pending is not None:
                    pending()
                bankgen(c)
                pending = (lambda c=c: conv_mm(c))
            pending()
            for hp in range(4):
                s = wnext("bv")
                vbv = wview(WS[s], KC, 256, 0)
                vi = hp % 2
                for tg in range(4):
                    tcol = 128 + (g * 4 + tg) * 128
                    b = genbank()
                    mmgroup(b, [(PB[b][:, 0:256], XT[:, k, tcol:tcol + 128], vbv[:, k, :], k == 0, k == KC - 1) for k in range(KC)],
                            reads=xt_res(tcol, tcol + 128) + [rWS[s]])
                    op("act", lambda e, b=b, tg=tg: e.activation(out=V32[tg], in_=PB[b][:, 0:256], func=AF.Gelu_apprx_tanh),
                       reads=[pbres[b]], writes=[rV32[tg]])
                    for hh in range(2):
                        sl = slice(hh * 128, (hh + 1) * 128)
                        col = tg * 2 + hh
                        op("dve", lambda e, tg=tg, sl=sl: e.bn_stats(out=st[:, 16:22], in_=V32[tg][:, sl]), reads=[rV32[tg]], writes=[rST])
                        op("dve", lambda e, col=col: e.bn_aggr(out=st[:, 32 + 2 * col:34 + 2 * col], in_=st[:, 16:22]), reads=[rST], writes=[rST])
                mv = st[:, 32:48].rearrange("p (c two) -> p c two", two=2)
                op("act", lambda e: e.activation(out=st[:, 72:80], in_=mv[:, :, 1], func=AF.Sqrt, bias=epsln[:, 1:2], scale=1.0),
                   reads=[rST, rC2], writes=[rST])
                op("dve", lambda e: e.reciprocal(out=st[:, 80:88], in_=st[:, 72:80]), reads=[rST], writes=[rST])
                op("dve", lambda e: e.scalar_tensor_tensor(out=st[:, 88:96], in0=mv[:, :, 0], scalar=-1.0, in1=st[:, 80:88],
                                                           op0=ALU.mult, op1=ALU.mult), reads=[rST], writes=[rST])
                for tg in range(4):
                    for hh in range(2):
                        sl = slice(hh * 128, (hh + 1) * 128)
                        col = tg * 2 + hh
                        op("act", lambda e, tg=tg, sl=sl, col=col, vi=vi: e.activation(out=VB[vi][:, tg, sl], in_=V32[tg][:, sl], func=AF.Identity,
                                                                                     bias=st[:, 88 + col:89 + col], scale=st[:, 80 + col:81 + col]),
                           reads=[rST, rV32[tg]], writes=[rVB[vi]])
                s2 = wnext("bu")
                vbu = wview(WS[s2], KC, 256, 0)
                for hh in range(2):
                    h = hp * 2 + hh
                    b = genbank()
                    mc0 = 128 + g * 512
                    mmgroup(b, [(PB[b][:, 0:512], vbu[:, k, hh * 128:(hh + 1) * 128], XT[:, k, mc0:mc0 + 512], k == 0, k == KC - 1)
                                for k in range(KC)], reads=xt_res(mc0, mc0 + 512) + [rWS[s2]])
                    ui = h % 2
                    op("act", lambda e, b=b, ui=ui: e.activation(out=UT[ui], in_=PB[b][:, 0:512], func=AF.Gelu_apprx_tanh),
                       reads=[pbres[b]], writes=[rUT[ui]])
                    b2 = genbank()
                    mmgroup(b2, [(PB[b2][:, tg * 128:(tg + 1) * 128], VB[vi][:, tg, hh * 128:(hh + 1) * 128], WmT[:, h, :], True, True)
                                 for tg in range(4)], reads=[rVB[vi], rWm])
                    tmp = TMPr[:, ui * 512:(ui + 1) * 512]
                    op("dve", lambda e, b2=b2, h=h, tmp=tmp: e.scalar_tensor_tensor(
                        out=tmp.rearrange("p (c t) -> p c t", c=4), in0=PB[b2][:, :].rearrange("p (c t) -> p c t", c=4),
                        scalar=sgT[:, h:h + 1], in1=BB2[:, h, :].unsqueeze(1).broadcast_to([128, 4, 128]), op0=ALU.mult, op1=ALU.add),
                       reads=[pbres[b2], rC, rWm], writes=[rTMP[ui]])
                    op("dve", lambda e, ui=ui, h=h, tmp=tmp: e.tensor_tensor(out=YT[:, 8 + h, :], in0=tmp, in1=UT[ui], op=ALU.mult),
                       reads=[rTMP[ui], rUT[ui]], writes=[rYT])
            bS = genbank()
            mmgroup(bS, [(PB[bS][:, :], ones32[:, :], COv(c), c == 0, c == 7) for c in range(8)], reads=[rCO, rC2])
            bQ = genbank()
            for c in range(8):
                qi = c % 2
                sq = TMPr[:, qi * 512:(qi + 1) * 512]
                op("act", lambda e, sq=sq, c=c: e.activation(out=sq, in_=COv(c), func=AF.Square), reads=[rCO], writes=[rTMP[qi]])
                mmgroup(bQ, [(PB[bQ][:, :], ones32[:, :], sq, c == 0, c == 7)], reads=[rTMP[qi], rC2])
            T0 = TMPr[:, 0:512]
            T1 = TMPr[:, 512:1024]
            op("act", lambda e, bS=bS: e.activation(out=PB[bS][:, :], in_=PB[bS][:, :], func=AF.Copy, scale=1.0 / 1024),
               writes=[pbres[bS]])
            op("act", lambda e, bS=bS: e.activation(out=T0, in_=PB[bS][:, :], func=AF.Square), reads=[pbres[bS]], writes=[rTMP[0]])
            op("dve", lambda e, bQ=bQ: e.scalar_tensor_tensor(out=PB[bQ][:, :], in0=PB[bQ][:, :], scalar=1.0 / 1024, in1=T0,
                                                              op0=ALU.mult, op1=ALU.subtract), reads=[rTMP[0]], writes=[pbres[bQ]])
            op("act", lambda e, bQ=bQ: e.activation(out=PB[bQ][:, :], in_=PB[bQ][:, :], func=AF.Sqrt, bias=epsln[:, 1:2], scale=1.0),
               reads=[rC2], writes=[pbres[bQ]])
            op("dve", lambda e, bQ=bQ: e.reciprocal(out=T1, in_=PB[bQ][:, :]), reads=[pbres[bQ]], writes=[rTMP[1]])
            op("dve", lambda e, bS=bS: e.scalar_tensor_tensor(out=PB[bS][:, :], in0=PB[bS][:, :], scalar=-1.0, in1=T1,
                                                              op0=ALU.mult, op1=ALU.mult), reads=[rTMP[1]], writes=[pbres[bS]])
            for c in range(8):
                op("dve", lambda e, c=c: e.tensor_tensor(out=COv(c), in0=COv(c), in1=T1, op=ALU.mult),
                   reads=[rTMP[1]], writes=[rCO])
                op("dve", lambda e, c=c, bS=bS: e.tensor_tensor(out=COv(c), in0=COv(c), in1=PB[bS][:, :], op=ALU.add),
                   reads=[pbres[bS]], writes=[rCO])
                op("act", lambda e, c=c: e.activation(out=YT[:, c, :], in_=COv(c), func=AF.Silu,
                                                      bias=cvec[:, 16 + c:17 + c], scale=cvec[:, 8 + c:9 + c]),
                   reads=[rCO, rC], writes=[rYT])
            for nb in range(8):
                s = wnext("out")
                vo = wview(WS[s], KC, 256, 0)
                for tg in range(4):
                    t = 1 + g * 4 + tg
                    b = genbank()
                    mmgroup(b, [(PB[b][:, 0:256], YT[:, kk, tg * 128:(tg + 1) * 128], vo[:, kk, :], kk == 0, kk == KC - 1)
                                for kk in range(KC)], reads=[rYT, rWS[s]])
                    src, rs = rtile(t)
                    o = src[:, nb * 256:(nb + 1) * 256]
                    op("dve", lambda e, b=b, o=o: e.scalar_tensor_tensor(out=o, in0=PB[b][:, 0:256], scalar=inv_a, in1=o,
                                                                          op0=ALU.mult, op1=ALU.add),
                       reads=[pbres[b]], writes=[rs])
                    op("dve", lambda e, o=o, t=t, nb=nb: e.bn_stats(out=STT[:, t, nb * 6:(nb + 1) * 6], in_=o),
                       reads=[rs], writes=[rSTT[t]])
        for r_ in rVB + rTMP + [rCO]:
            inherit(rGB, r_)
        inherit(rHT, rYT, rCO)

        layer_norm(list(range(1, 9)), 1)
        ffn(1, list(range(1, 9)), 2)
        layer_norm(list(range(1, 9)), 2, final=True)

        P.emit("sp", lambda e: e.nop(), [otoks[-1]])

        P.build(nc, sems)
    return nc


_NC_CACHE = {}


def kernel(x, ffn1_w_gate_up, ffn1_w_down, ln1_g, ln1_b, mix_w_in, conv_w, conv_b, conv_ln_g, conv_ln_b,
           sg_ln_g, sg_ln_b, sg_w, sg_b, mix_w_out, ln2_g, ln2_b, ffn2_w_gate_up, ffn2_w_down, ln3_g, ln3_b):
    f = lambda a: np.ascontiguousarray(np.asarray(a, dtype=np.float32))
    x = f(x)
    common = {
        "w_gu1": f(ffn1_w_gate_up)[0], "w_gu2": f(ffn2_w_gate_up)[0],
        "w_dn1": f(ffn1_w_down)[0], "w_dn2": f(ffn2_w_down)[0],
        "w_in": f(mix_w_in)[0], "w_out": f(mix_w_out)[0],
        "ln1_g": f(ln1_g).reshape(1, D), "ln1_b": f(ln1_b).reshape(1, D),
        "ln2_g": f(ln2_g).reshape(1, D), "ln2_b": f(ln2_b).reshape(1, D),
        "ln3_g": f(ln3_g).reshape(1, D), "ln3_b": f(ln3_b).reshape(1, D),
        "cw": np.ascontiguousarray(f(conv_w)[0].reshape(31, 8, 128).transpose(2, 1, 0).reshape(128, 8 * 31)),
        "cvec": np.ascontiguousarray(np.concatenate([f(conv_b)[0].reshape(8, 128).T, f(conv_ln_g)[0].reshape(8, 128).T,
                                                     f(conv_ln_b)[0].reshape(8, 128).T], axis=1)),
        "sgT": np.ascontiguousarray(np.concatenate([f(sg_ln_g).reshape(8, 128).T, f(sg_ln_b).reshape(8, 128).T], axis=1)),
        "sgw": f(sg_w)[0], "sgbias": f(sg_b).reshape(1, 1024),
        "ident": np.eye(128, dtype=np.float32),
        "maskT": np.triu(np.ones((128, 128), dtype=np.float32)),
    }
    in_maps = []
    for c in range(NCORES):
        b, s0 = c // 4, (c % 4) * TOK
        m = dict(common)
        m["x_main"] = np.ascontiguousarray(x[b, s0:s0 + TOK])
        if s0 > 0:
            m["x_halo"] = np.ascontiguousarray(x[b, s0 - HALO:s0])
            m["hmask"] = np.ones((128, 1), np.float32)
        else:
            m["x_halo"] = np.zeros((HALO, D), np.float32)
            m["hmask"] = np.zeros((128, 1), np.float32)
        in_maps.append(m)
    if "nc" not in _NC_CACHE:
        _NC_CACHE["nc"] = build_program()
    nc = _NC_CACHE["nc"]
    res = run_bass_kernel_spmd(nc, in_maps, core_ids=list(range(NCORES)))
    out = np.empty((2, 4096, D), np.float32)
    for c in range(NCORES):
        b, s0 = c // 4, (c % 4) * TOK
        out[b, s0:s0 + TOK] = res.results[c]["out"]
    return out
```

```python
from contextlib import ExitStack
import numpy as np
import concourse.bass as bass
import concourse.mybir as mybir
from concourse.bass_utils import run_bass_kernel_spmd

F32 = mybir.dt.float32
F32R = mybir.dt.float32r
BF16 = mybir.dt.bfloat16
AF = mybir.ActivationFunctionType
ALU = mybir.AluOpType
AX = mybir.AxisListType

NCORES = 8
D = 2048
DFF = 5632
TOK = 1024
HALO = 128
NT = TOK + HALO
KC = D // 128
NQ = 4
JQ = DFF // 128 // NQ
ALPHA = 2.0 ** 0.25
LN_EPS = 1e-5
ENGS = ["pe", "act", "dve", "pool", "sp"]
ESEM = {"pe": "pe", "act": "act", "dve": "dve", "pool": "pool"}


class Prog:
    def __init__(self):
        self.ops = {e: [] for e in ENGS}
        self.cnt = {}
        self.waited = {e: {} for e in ENGS}

    def newsem(self, name):
        self.cnt[name] = 0
        return name

    def emit(self, eng, fn, deps=(), sem=None, inc=1):
        waits = []
        for d in deps:
            if d is None:
                continue
            s, v = d
            if self.waited[eng].get(s, 0) >= v:
                continue
            self.waited[eng][s] = v
            waits.append((s, v))
        tok = None
        if sem is not None:
            self.cnt[sem] += inc
            tok = (sem, self.cnt[sem])
        self.ops[eng].append((waits, fn, sem, inc))
        return tok

    def build(self, nc, sems):
        handles = {"pe": "tensor", "act": "scalar", "dve": "vector", "pool": "gpsimd", "sp": "sync"}
        with nc.Block() as block:
            for e in ENGS:
                ops = self.ops[e]

                def body(engine, ops=ops):
                    for waits, fn, sem, inc in ops:
                        for s, v in waits:
                            engine.wait_ge(sems[s], v)
                        ins = fn(engine)
                        if sem is not None:
                            ins.then_inc(sems[sem], inc)

                getattr(block, handles[e])(body)


class Res:
    def __init__(self):
        self.w = None
        self.r = {}

    def rd(self):
        return [self.w]

    def wr(self):
        return [self.w] + [(s, v) for s, v in self.r.items()]

    def did_read(self, t):
        if t is not None:
            s, v = t
            self.r[s] = max(self.r.get(s, 0), v)

    def did_write(self, t):
        self.w = t
        self.r = {}


def build_program(debug=False):
    nc = bass.Bass("TRN2", target_bir_lowering=False)
    P = Prog()

    def din(name, shape):
        return nc.dram_tensor(name, list(shape), F32, kind="ExternalInput").ap()

    x_main = din("x_main", [TOK, D])
    x_halo = din("x_halo", [HALO, D])
    hmask = din("hmask", [128, 1])
    w_gu = [din("w_gu1", [D, 2 * DFF]), din("w_gu2", [D, 2 * DFF])]
    w_dn = [din("w_dn1", [DFF, D]), din("w_dn2", [DFF, D])]
    w_in = din("w_in", [D, 4096])
    w_out = din("w_out", [D, D])
    ln_g = [din(f"ln{i}_g", [1, D]) for i in (1, 2, 3)]
    ln_b = [din(f"ln{i}_b", [1, D]) for i in (1, 2, 3)]
    cw_d = din("cw", [128, 8 * 31])
    cvec_d = din("cvec", [128, 24])
    sgT_d = din("sgT", [128, 16])
    sgw_d = din("sgw", [8, 128, 128])
    sgbias_d = din("sgbias", [1, 1024])
    ident_d = din("ident", [128, 128])
    maskT_d = din("maskT", [128, 128])
    out_d = nc.dram_tensor("out", [TOK, D], F32, kind="ExternalOutput").ap()

    with ExitStack() as es:
        def sb(name, shape, dt):
            return es.enter_context(nc.sbuf_tensor("s_" + name, shape, dt))

        def ps(name):
            return es.enter_context(nc.psum_tensor(name, [128, 512], F32))

        R = [sb(f"R{t}", [128, D], F32) for t in range(8)]
        HX = sb("HX", [128, D], F32)
        XT = sb("XT", [128, KC, NT], BF16)
        HTr = sb("HTr", [128, JQ * NT], BF16)
        GBr = sb("GBr", [128, 2 * D], F32)
        HCr = sb("HCr", [128, 1280], F32)
        BANK = sb("BANK", [128, 31 * 128], F32)
        WS = [sb(f"WS{i}", [128, 4096], BF16) for i in range(3)]
        ident = sb("ident", [128, 128], F32)
        maskT = sb("maskT", [128, 128], F32)
        ones32 = sb("ones32", [128, 128], F32)
        WmT = sb("WmT", [128, 8, 128], BF16)
        BB2 = sb("BB2", [128, 8, 128], F32)
        sgT = sb("sgT", [128, 16], F32)
        ones_bf = sb("ones_bf", [128, 128], BF16)
        junk = sb("junk", [128, 128], F32)
        cw = sb("cw", [128, 8 * 31], F32)
        cvec = sb("cvec", [128, 24], F32)
        hm = sb("hm", [128, 1], F32)
        st = sb("st", [128, 128], F32)
        STT = sb("STT", [128, 9, 48], F32)
        epsln = sb("epsln", [128, 2], F32)

        hT = HTr[:, :].rearrange("p (j t) -> p j t", j=JQ)
        YT = HTr[:, 0:16 * 512].rearrange("p (k t) -> p k t", k=16)
        g_bc = GBr[:, 0:D]
        b_bc = GBr[:, D:2 * D]
        TMPr = GBr[:, 3072:4096]
        VBr = GBr[:, 2048:3072].bitcast(BF16)
        VB = [VBr[:, i * 1024:(i + 1) * 1024].rearrange("p (t d) -> p t d", t=4) for i in range(2)]
        UT = [HX[:, i * 512:(i + 1) * 512] for i in range(2)]
        V32 = [HX[:, 1024 + i * 256:1024 + (i + 1) * 256] for i in range(4)]
        HC = [HCr[:, i * 640:(i + 1) * 640] for i in range(2)]
        bank = BANK[:, :].rearrange("p (k n) -> p k n", k=31)
        sgw32 = GBr[:, 0:1024].rearrange("p (h s) -> p h s", h=8)
        bbias = GBr[:, 1024:2048].rearrange("p (h t) -> p h t", h=8)

        def COv(c):
            if c < 4:
                return GBr[:, c * 512:(c + 1) * 512]
            return HTr[:, 8192 + (c - 4) * 1024:8192 + (c - 3) * 1024].bitcast(F32)

        PB = [ps(f"pb{i}") for i in range(8)]
        pbres = [Res() for _ in range(8)]

        semnames = ["pe", "act", "dve", "pool", "xld", "cst", "gb", "ost", "sgl", "w0", "w1", "w2"]
        for n in semnames:
            P.newsem(n)
        sems = {n: es.enter_context(nc.semaphore(n)) for n in semnames}

        rR = [Res() for _ in range(8)]
        rHX = Res()
        rXT = [Res() for _ in range(9)]
        rHT = Res()
        rGB = Res()
        rTMP = [Res(), Res()]
        rCO = Res()
        rWS = [Res() for _ in range(3)]
        rC = Res()
        rC2 = Res()
        rWm = Res()
        rST = Res()
        rSTT = [Res() for _ in range(9)]
        rLS = [Res() for _ in range(4)]
        rUT = [Res(), Res()]
        rV32 = [Res() for _ in range(4)]
        rJ = Res()
        rSQ = [Res(), Res()]
        rVB = [Res(), Res()]
        rHC = [Res(), Res()]
        rYT = Res()

        def inherit(dst, *owners):
            for o in owners:
                for tk in o.wr():
                    dst.did_read(tk)

        def op(eng, fn, reads=(), writes=(), extra=()):
            deps = list(extra)
            for r in reads:
                deps += r.rd()
            for w in writes:
                deps += w.wr()
            tok = P.emit(eng, fn, deps, sem=ESEM[eng])
            for r in reads:
                r.did_read(tok)
            for w in writes:
                w.did_write(tok)
            return tok

        def dma(eng, out_ap, in_ap, sem, reads=(), writes=()):
            deps = []
            for r in reads:
                deps += r.rd()
            for w in writes:
                deps += w.wr()
            return P.emit(eng, lambda e: e.dma_start(out=out_ap, in_=in_ap), deps, sem=sem, inc=16)

        def mmgroup(bank, mms, reads):
            deps = pbres[bank].wr()
            for r in reads:
                deps += r.rd()
            tok = None
            n = len(mms)
            for i, (o, l, r_, st_, sp_) in enumerate(mms):
                tok = P.emit("pe", lambda e, o=o, l=l, r_=r_, st_=st_, sp_=sp_: e.matmul(o, l, r_, start=st_, stop=sp_),
                             deps if i == 0 else (), sem="pe" if i == n - 1 else None)
            pbres[bank].did_write(tok)
            for r in reads:
                r.did_read(tok)
            return tok

        gen_i = [0]

        def genbank():
            b = 4 + gen_i[0] % 4
            gen_i[0] += 1
            return b

        pair_i = [0]

        def pairbanks():
            s = pair_i[0] % 2
            pair_i[0] += 1
            return 2 * s, 2 * s + 1

        ws_i = [0]

        def wload(parts):
            s = ws_i[0] % 3
            ws_i[0] += 1
            deps = rWS[s].wr()
            tok = None
            for vf, src in parts:
                tok = P.emit("pool", lambda e, o=vf(WS[s]), i=src: e.dma_start(out=o, in_=i), deps, sem=f"w{s}", inc=16)
            rWS[s].did_write(tok)
            return s

        def wview(slot, k, n, off=0):
            return slot[:, off:off + k * n].rearrange("p (k n) -> p k n", k=k)

        wgu_v = [w.rearrange("(kc p) n -> p kc n", p=128) for w in w_gu]
        wdn_v = [w.rearrange("(j p) n -> p j n", p=128) for w in w_dn]
        win = w_in.rearrange("(kc p) n -> p kc n", p=128)
        wout = w_out.rearrange("(kc p) n -> p kc n", p=128)

        def wparts(spec):
            kind = spec[0]
            if kind == "gu":
                _, l, j = spec
                return [(lambda sl: wview(sl, KC, 128, 0), wgu_v[l][:, :, j * 128:(j + 1) * 128]),
                        (lambda sl: wview(sl, KC, 128, 2048), wgu_v[l][:, :, DFF + j * 128:DFF + (j + 1) * 128])]
            if kind == "dn":
                _, l, q, nb = spec
                return [(lambda sl: wview(sl, JQ, 256, 0), wdn_v[l][:, q * JQ:(q + 1) * JQ, nb * 256:(nb + 1) * 256])]
            i = spec[1]
            if kind == "a":
                return [(lambda sl: wview(sl, KC, 128, 0), win[:, :, 1024 + i * 128:1024 + (i + 1) * 128]),
                        (lambda sl: wview(sl, KC, 128, 2048), win[:, :, i * 128:(i + 1) * 128])]
            if kind == "bv":
                return [(lambda sl: wview(sl, KC, 256, 0), win[:, :, 3072 + i * 256:3072 + (i + 1) * 256])]
            if kind == "bu":
                return [(lambda sl: wview(sl, KC, 256, 0), win[:, :, 2048 + i * 256:2048 + (i + 1) * 256])]
            return [(lambda sl: wview(sl, KC, 256, 0), wout[:, :, i * 256:(i + 1) * 256])]

        specs = []

        def ffn_specs(l, split_first=False):
            for q in range(NQ):
                for rep in range(2 if (q == 0 and split_first) else 1):
                    specs.extend(("gu", l, q * JQ + jj) for jj in range(JQ))
                for rep in range(2 if q == NQ - 1 else 1):
                    specs.extend(("dn", l, q, nb) for nb in range(8))

        ffn_specs(0)
        for g in range(2):
            specs.extend(("a", c, g) for c in [4, 5, 6, 7, 0, 1, 2, 3])
            for hp in range(4):
                specs.extend([("bv", hp, g), ("bu", hp, g)])
            specs.extend(("out", nb, g) for nb in range(8))
        ffn_specs(1, split_first=True)
        wslots = {}
        wn = [0]
        wi = [0]

        def wprefetch(look=2):
            while wn[0] <= min(wi[0] + look, len(specs) - 1):
                wslots[wn[0]] = wload(wparts(specs[wn[0]]))
                wn[0] += 1

        def wnext(kind, look=2):
            i = wi[0]
            assert specs[i][0] == kind, (specs[i], kind)
            wprefetch(look)
            wi[0] += 1
            return wslots[i]

        ctoks = []
        for o, i in ((ident[:], ident_d), (maskT[:], maskT_d), (cw[:], cw_d), (cvec[:], cvec_d), (hm[:], hmask),
                     (sgT[:], sgT_d), (bbias, sgbias_d.partition_broadcast(128).rearrange("p o (h t) -> p (o h) t", h=8)), (sgw32, sgw_d.rearrange("h t s -> t h s"))):
            ctoks.append(dma("sp", o, i, "cst"))
        ctok = ctoks[-1]
        rC.did_write(ctok)
        rGB.did_write(ctok)
        t1 = op("dve", lambda e: e.memset(ones32[:], 1.0), writes=[rC2])
        t2 = op("dve", lambda e: e.memset(epsln[:, 0:1], LN_EPS / (ALPHA * ALPHA)), writes=[rC2])
        t3 = op("dve", lambda e: e.memset(epsln[:, 1:2], LN_EPS), writes=[rC2])
        t4 = op("dve", lambda e: e.memset(ones_bf[:], 1.0), writes=[rC2])

        xtoks = []
        for t in range(8):
            xtoks.append(dma("sp", R[t][:], x_main[t * 128:(t + 1) * 128, :], "xld"))
        xtoks.append(dma("sp", HX[:], x_halo, "xld"))
        for t in range(8):
            rR[t].did_write(xtoks[-1])
        rHX.did_write(xtoks[-1])

        for h in range(8):
            b = genbank()
            deps = pbres[b].wr() + rC.rd() + rGB.rd()
            tk = P.emit("pe", lambda e, b=b, h=h: e.transpose(PB[b][:, 0:128], sgw32[:, h, :], ident[:]), deps, sem="pe")
            pbres[b].did_write(tk)
            rGB.did_read(tk)
            op("dve", lambda e, b=b, h=h: e.tensor_tensor(out=WmT[:, h, :], in0=PB[b][:, 0:128], in1=maskT[:], op=ALU.mult),
               reads=[pbres[b], rC], writes=[rWm])
            b2 = genbank()
            mmgroup(b2, [(PB[b2][:, 0:128], ones_bf[:, :], WmT[:, h, :], True, True)], reads=[rWm, rC2])
            tk2 = op("dve", lambda e, b2=b2, h=h: e.scalar_tensor_tensor(out=BB2[:, h, :], in0=PB[b2][:, 0:128], scalar=sgT[:, 8 + h:9 + h],
                                                                   in1=bbias[:, h, :], op0=ALU.mult, op1=ALU.add),
                     reads=[pbres[b2], rC, rGB], writes=[rWm])

        def rtile(t):
            if t == 0:
                return HX, rHX
            return R[t - 1], rR[t - 1]

        def transpose_tile(t):
            src, rs = rtile(t)
            for fb in range(4):
                b = genbank()
                deps = pbres[b].wr() + rs.rd() + rC.rd()
                tk = None
                for i in range(4):
                    f = fb * 4 + i
                    tk = P.emit("pe", lambda e, b=b, i=i, f=f, src=src: e.transpose(
                        PB[b][:, i * 128:(i + 1) * 128], src[:, f * 128:(f + 1) * 128], ident[:]),
                        deps if i == 0 else (), sem="pe" if i == 3 else None)
                pbres[b].did_write(tk)
                rs.did_read(tk)
                eng = "act" if fb % 2 == 0 else "dve"
                o = XT[:, fb * 4:fb * 4 + 4, t * 128:(t + 1) * 128]
                i_ = PB[b][:, :].rearrange("p (f n) -> p f n", f=4)
                if eng == "act":
                    op("act", lambda e, o=o, i_=i_: e.activation(out=o, in_=i_, func=AF.Copy), reads=[pbres[b]], writes=[rXT[t]])
                else:
                    op("dve", lambda e, o=o, i_=i_: e.tensor_copy(out=o, in_=i_), reads=[pbres[b]], writes=[rXT[t]])

        def xt_res(c0, c1):
            return [rXT[t] for t in range(c0 // 128, (c1 + 127) // 128)]

        def pair_block(slot, lA, lB, ncols, func, out_fn, out_res, ws_idx, tmps=None):
            for (c0, c1) in ncols:
                ba, bb = pairbanks()
                n = c1 - c0
                xr = xt_res(c0, c1)
                mmgroup(ba, [(PB[ba][:, 0:n], lA(k), XT[:, k, c0:c1], k == 0, k == KC - 1) for k in range(KC)],
                        reads=xr + [rWS[ws_idx]])
                mmgroup(bb, [(PB[bb][:, 0:n], lB(k), XT[:, k, c0:c1], k == 0, k == KC - 1) for k in range(KC)],
                        reads=xr + [rWS[ws_idx]])
                ti = pair_i[0] % 2
                if tmps is None:
                    tmp, rtmp = TMPr[:, ti * 512:ti * 512 + n], rTMP[ti]
                    inherit(rtmp, rGB)
                else:
                    tmp, rtmp = tmps[0][ti][:, 0:n], tmps[1][ti]
                op("act", lambda e, tmp=tmp, ba=ba, n=n: e.activation(out=tmp, in_=PB[ba][:, 0:n], func=func),
                   reads=[pbres[ba]], writes=[rtmp])
                op("dve", lambda e, tmp=tmp, bb=bb, n=n, o=out_fn(c0, c1): e.tensor_tensor(out=o, in0=tmp, in1=PB[bb][:, 0:n], op=ALU.mult),
                   reads=[pbres[bb], rtmp], writes=[out_res])

        otoks = []

        def out_dma(t):
            otoks.append(dma("sp", out_d[(t - 1) * 128:t * 128, :], R[t - 1][:], "ost", reads=[rR[t - 1]]))

        def make_ln(tiles, li, final=False, load=True):
            if load:
                inherit(rGB, rTMP[0], rTMP[1])
                t_a = dma("sp", g_bc, ln_g[li].partition_broadcast(128), "gb", writes=[rGB])
                t_b = dma("sp", b_bc, ln_b[li].partition_broadcast(128), "gb", writes=[rGB])
                rGB.did_write(t_b)
            def S1(t):
                src, rs = rtile(t)
                c0 = (t % 4) * 4
                rl = rLS[t % 4]
                op("dve", lambda e, t=t, c0=c0: e.bn_aggr(out=st[:, c0:c0 + 2], in_=STT[:, t, :]), reads=[rSTT[t]], writes=[rl])
                op("act", lambda e, c0=c0: e.activation(out=st[:, c0 + 2:c0 + 3], in_=st[:, c0 + 1:c0 + 2], func=AF.Sqrt,
                                                        bias=epsln[:, 0:1], scale=1.0), reads=[rl, rC2], writes=[rl])
                op("dve", lambda e, c0=c0: e.reciprocal(out=st[:, c0 + 3:c0 + 4], in_=st[:, c0 + 2:c0 + 3]), reads=[rl], writes=[rl])
                op("dve", lambda e, src=src, c0=c0: e.scalar_tensor_tensor(out=src[:, :], in0=src[:, :], scalar=st[:, c0:c0 + 1], in1=g_bc,
                                                                            op0=ALU.subtract, op1=ALU.mult),
                   reads=[rl, rGB], writes=[rs])

            def S2(t):
                src, rs = rtile(t)
                c0 = (t % 4) * 4
                op("act", lambda e, src=src, c0=c0: e.activation(out=src[:, :], in_=src[:, :], func=AF.Identity, bias=0.0,
                                                                 scale=st[:, c0 + 3:c0 + 4]), reads=[rLS[t % 4]], writes=[rs])

            def S3(t):
                src, rs = rtile(t)
                op("pool", lambda e, src=src: e.tensor_tensor(out=src[:, :], in0=src[:, :], in1=b_bc, op=ALU.add), reads=[rGB], writes=[rs])

            def S4(t):
                if not final:
                    transpose_tile(t)
                else:
                    out_dma(t)

            stages = [S1, S2, S3, S4]
            n = len(tiles)
            steps = []
            for i in range(n + len(stages) - 1):
                def step(i=i):
                    wprefetch()
                    for si, fn_ in enumerate(stages):
                        k = i - si
                        if 0 <= k < n:
                            fn_(tiles[k])
                steps.append(step)
            return steps

        def run_steps(steps):
            while steps:
                steps.pop(0)()

        def ffn(l, tiles, li, split_first=False, tmps=None, pre_steps=None, final=False):
            c_lo = tiles[0] * 128
            c_hi = (tiles[-1] + 1) * 128
            ncols_tot = c_hi - c_lo
            nch = 3 if ncols_tot == NT else 2
            step = ncols_tot // nch
            ncols = [(c_lo + i * step, c_lo + (i + 1) * step) for i in range(nch)]
            coef = 0.5 / ALPHA
            pre_steps = pre_steps if pre_steps is not None else []
            nA = (len(tiles) + 1) // 2
            tA, tB = tiles[:nA], tiles[nA:]

            def gate_up(jj, cols):
                s = wnext("gu")
                vg = wview(WS[s], KC, 128, 0)
                vu = wview(WS[s], KC, 128, 2048)
                pair_block(WS[s], lambda k, vg=vg: vg[:, k, :], lambda k, vu=vu: vu[:, k, :], cols, AF.Silu,
                           lambda c0, c1, jj=jj: hT[:, jj, c0:c1], rHT, s, tmps=tmps)

            def down(q, nb, tl):
                s = wnext("dn")
                vd = wview(WS[s], JQ, 256, 0)
                for t in tl:
                    b = genbank()
                    mmgroup(b, [(PB[b][:, 0:256], hT[:, jj, t * 128:(t + 1) * 128], vd[:, jj, :], jj == 0, jj == JQ - 1)
                                for jj in range(JQ)], reads=[rHT, rWS[s]])
                    src, rs = rtile(t)
                    o = src[:, nb * 256:(nb + 1) * 256]
                    op("dve", lambda e, b=b, o=o: e.scalar_tensor_tensor(out=o, in0=PB[b][:, 0:256], scalar=coef, in1=o,
                                                                          op0=ALU.mult, op1=ALU.add),
                       reads=[pbres[b]], writes=[rs])
                    if q == NQ - 1:
                        op("dve", lambda e, o=o, t=t, nb=nb: e.bn_stats(out=STT[:, t, nb * 6:(nb + 1) * 6], in_=o),
                           reads=[rs], writes=[rSTT[t]])

            for q in range(NQ):
                if q == 0 and split_first:
                    half_cols = [[ncols[0]], [ncols[1]]]
                    for hc_ in half_cols:
                        for jj in range(JQ):
                            gate_up(jj, hc_)
                            if pre_steps:
                                pre_steps.pop(0)()
                    run_steps(pre_steps)
                else:
                    for jj in range(JQ):
                        gate_up(jj, ncols)
                if q < NQ - 1:
                    for nb in range(8):
                        down(q, nb, tiles)
                else:
                    for nb in range(8):
                        down(q, nb, tA)
                    stepsA = make_ln(tA, li, final=final)
                    for nb in range(8):
                        down(q, nb, tB)
                        if stepsA:
                            stepsA.pop(0)()
                    run_steps(stepsA)
                    return make_ln(tB, li, final=final, load=False)

        for t in range(9):
            transpose_tile(t)

        ln_steps = ffn(0, list(range(9)), 0)


        rBank = Res()
        for r_ in rUT + rV32:
            inherit(r_, rHX)
        for r_ in rVB + rTMP:
            inherit(r_, rGB)
        inherit(rCO, rGB, rHT)
        inherit(rYT, rHT)
        inv_a = 1.0 / ALPHA

        def tap_ap(k):
            return bank[:, k, :]

        def bankgen(c):
            op("pool", lambda e, c=c: e.tensor_tensor(
                out=bank.bitcast(F32R), in0=ident[:, :].unsqueeze(1).broadcast_to([128, 31, 128]),
                in1=cw[:, c * 31:c * 31 + 31].unsqueeze(2).broadcast_to([128, 31, 128]), op=ALU.mult),
               reads=[rC], writes=[rBank])

        def conv_mm(c, hi):
            hcr = HC[hi].bitcast(F32R)
            b = genbank()
            mmgroup(b, [(PB[b][:, 0:512], tap_ap(k).bitcast(F32R), hcr[:, 98 + k:610 + k], k == 0, k == 30) for k in range(31)],
                    reads=[rHC[hi], rBank])
            op("act", lambda e, b=b, c=c: e.activation(out=COv(c), in_=PB[b][:, 0:512], func=AF.Identity,
                                                       bias=cvec[:, c:c + 1], scale=1.0),
               reads=[pbres[b], rC], writes=[rCO])

        mix_tmps = (UT, rUT)
        CORDER = [4, 5, 6, 7, 0, 1, 2, 3]
        for g in range(2):
            gc0 = g * 512
            pending = None
            for ci, c in enumerate(CORDER):
                s = wnext("a")
                assert specs[wi[0] - 1][1] == c
                vgt = wview(WS[s], KC, 128, 0)
                vvl = wview(WS[s], KC, 128, 2048)
                hi = ci % 2
                hcr = HC[hi].bitcast(F32R)
                pair_block(WS[s], lambda k, v=vgt: v[:, k, :], lambda k, v=vvl: v[:, k, :],
                           [(gc0, gc0 + 320), (gc0 + 320, gc0 + 640)], AF.Sigmoid,
                           lambda c0, c1, hcr=hcr, gc0=gc0: hcr[:, c0 - gc0:c1 - gc0], rHC[hi], s, tmps=mix_tmps)
                if g == 0:
                    op("dve", lambda e, hi=hi, hcr=hcr: e.tensor_scalar(out=hcr[:, 0:128], in0=HC[hi][:, 0:128], scalar1=hm[:, 0:1],
                                                                         scalar2=None, op0=ALU.mult), reads=[rC], writes=[rHC[hi]])
                if ci == 5:
                    run_steps(ln_steps)
                    inherit(rCO, rGB)
                if pending is not None:
                    pending()
                bankgen(c)
                pending = (lambda c=c, hi=hi: conv_mm(c, hi))
                for _ in range(2):
                    if ln_steps:
                        ln_steps.pop(0)()
            pending()
            run_steps(ln_steps)
            for r_ in rVB + rTMP:
                inherit(r_, rGB)

            T0 = TMPr[:, 0:512]
            T1 = TMPr[:, 512:1024]
            cl = {}

            def convln_sums():
                bS, bQ = pairbanks()
                mmgroup(bS, [(PB[bS][:, :], ones32[:, :], COv(c), c == 0, c == 7) for c in range(8)], reads=[rCO, rC2])
                for c in range(8):
                    qi = c % 2
                    sq = TMPr[:, qi * 512:(qi + 1) * 512]
                    op("act", lambda e, sq=sq, c=c: e.activation(out=sq, in_=COv(c), func=AF.Square), reads=[rCO], writes=[rTMP[qi]])
                    mmgroup(bQ, [(PB[bQ][:, :], ones32[:, :], sq, c == 0, c == 7)], reads=[rTMP[qi], rC2])
                cl["bS"], cl["bQ"] = bS, bQ

            def convln_chain():
                bS, bQ = cl["bS"], cl["bQ"]
                op("act", lambda e: e.activation(out=PB[bS][:, :], in_=PB[bS][:, :], func=AF.Copy, scale=1.0 / 1024),
                   writes=[pbres[bS]])
                op("act", lambda e: e.activation(out=T0, in_=PB[bS][:, :], func=AF.Square), reads=[pbres[bS]], writes=[rTMP[0]])
                op("dve", lambda e: e.scalar_tensor_tensor(out=PB[bQ][:, :], in0=PB[bQ][:, :], scalar=1.0 / 1024, in1=T0,
                                                           op0=ALU.mult, op1=ALU.subtract), reads=[rTMP[0]], writes=[pbres[bQ]])
                op("act", lambda e: e.activation(out=PB[bQ][:, :], in_=PB[bQ][:, :], func=AF.Sqrt, bias=epsln[:, 1:2], scale=1.0),
                   reads=[rC2], writes=[pbres[bQ]])
                op("dve", lambda e: e.reciprocal(out=T1, in_=PB[bQ][:, :]), reads=[pbres[bQ]], writes=[rTMP[1]])
                op("dve", lambda e: e.scalar_tensor_tensor(out=PB[bS][:, :], in0=PB[bS][:, :], scalar=-1.0, in1=T1,
                                                           op0=ALU.mult, op1=ALU.mult), reads=[rTMP[1]], writes=[pbres[bS]])

            def convln_apply(c):
                bS = cl["bS"]
                op("dve", lambda e: e.tensor_tensor(out=COv(c), in0=COv(c), in1=T1, op=ALU.mult), reads=[rTMP[1]], writes=[rCO])
                op("dve", lambda e: e.tensor_tensor(out=COv(c), in0=COv(c), in1=PB[bS][:, :], op=ALU.add), reads=[pbres[bS]], writes=[rCO])
                op("act", lambda e: e.activation(out=YT[:, c, :], in_=COv(c), func=AF.Silu,
                                                 bias=cvec[:, 16 + c:17 + c], scale=cvec[:, 8 + c:9 + c]),
                   reads=[rCO, rC], writes=[rYT])

            convln_parts = [convln_chain] + [(lambda c=c: convln_apply(c)) for c in range(8)]
            convln_sums()

            for hp in range(4):
                s = wnext("bv")
                vbv = wview(WS[s], KC, 256, 0)
                vi = hp % 2
                for tg in range(4):
                    tcol = 128 + (g * 4 + tg) * 128
                    b = genbank()
                    mmgroup(b, [(PB[b][:, 0:256], XT[:, k, tcol:tcol + 128], vbv[:, k, :], k == 0, k == KC - 1) for k in range(KC)],
                            reads=xt_res(tcol, tcol + 128) + [rWS[s]])
                    op("act", lambda e, b=b, tg=tg: e.activation(out=V32[tg], in_=PB[b][:, 0:256], func=AF.Gelu_apprx_tanh),
                       reads=[pbres[b]], writes=[rV32[tg]])
                    for hh in range(2):
                        sl = slice(hh * 128, (hh + 1) * 128)
                        col = tg * 2 + hh
                        op("dve", lambda e, tg=tg, sl=sl: e.bn_stats(out=st[:, 16:22], in_=V32[tg][:, sl]), reads=[rV32[tg]], writes=[rST])
                        op("dve", lambda e, col=col: e.bn_aggr(out=st[:, 32 + 2 * col:34 + 2 * col], in_=st[:, 16:22]), reads=[rST], writes=[rST])
                s2 = wnext("bu")
                vbu = wview(WS[s2], KC, 256, 0)
                mc0 = 128 + g * 512
                ub = []
                for hh in range(2):
                    b = genbank()
                    mmgroup(b, [(PB[b][:, 0:512], vbu[:, k, hh * 128:(hh + 1) * 128], XT[:, k, mc0:mc0 + 512], k == 0, k == KC - 1)
                                for k in range(KC)], reads=xt_res(mc0, mc0 + 512) + [rWS[s2]])
                    ub.append(b)
                mv = st[:, 32:48].rearrange("p (c two) -> p c two", two=2)
                op("act", lambda e: e.activation(out=st[:, 72:80], in_=mv[:, :, 1], func=AF.Sqrt, bias=epsln[:, 1:2], scale=1.0),
                   reads=[rST, rC2], writes=[rST])
                op("dve", lambda e: e.reciprocal(out=st[:, 80:88], in_=st[:, 72:80]), reads=[rST], writes=[rST])
                op("dve", lambda e: e.scalar_tensor_tensor(out=st[:, 88:96], in0=mv[:, :, 0], scalar=-1.0, in1=st[:, 80:88],
                                                           op0=ALU.mult, op1=ALU.mult), reads=[rST], writes=[rST])
                for tg in range(4):
                    for hh in range(2):
                        sl = slice(hh * 128, (hh + 1) * 128)
                        col = tg * 2 + hh
                        op("act", lambda e, tg=tg, sl=sl, col=col, vi=vi: e.activation(out=VB[vi][:, tg, sl], in_=V32[tg][:, sl], func=AF.Identity,
                                                                                     bias=st[:, 88 + col:89 + col], scale=st[:, 80 + col:81 + col]),
                           reads=[rST, rV32[tg]], writes=[rVB[vi]])
                for hh in range(2):
                    h = hp * 2 + hh
                    ui = h % 2
                    op("act", lambda e, b=ub[hh], ui=ui: e.activation(out=UT[ui], in_=PB[b][:, 0:512], func=AF.Gelu_apprx_tanh),
                       reads=[pbres[ub[hh]]], writes=[rUT[ui]])
                for hh in range(2):
                    h = hp * 2 + hh
                    ui = h % 2
                    b2 = genbank()
                    mmgroup(b2, [(PB[b2][:, tg * 128:(tg + 1) * 128], VB[vi][:, tg, hh * 128:(hh + 1) * 128], WmT[:, h, :], True, True)
                                 for tg in range(4)], reads=[rVB[vi], rWm])
                    tmp = HX[:, 1024 + hh * 512:1024 + (hh + 1) * 512]
                    rt = [rV32[2 * hh], rV32[2 * hh + 1]]
                    op("dve", lambda e, b2=b2, h=h, tmp=tmp: e.scalar_tensor_tensor(
                        out=tmp.rearrange("p (c t) -> p c t", c=4), in0=PB[b2][:, :].rearrange("p (c t) -> p c t", c=4),
                        scalar=sgT[:, h:h + 1], in1=BB2[:, h, :].unsqueeze(1).broadcast_to([128, 4, 128]), op0=ALU.mult, op1=ALU.add),
                       reads=[pbres[b2], rC, rWm], writes=rt)
                    op("dve", lambda e, ui=ui, h=h, tmp=tmp: e.tensor_tensor(out=YT[:, 8 + h, :], in0=tmp, in1=UT[ui], op=ALU.mult),
                       reads=rt + [rUT[ui]], writes=[rYT])
                for _ in range(3 if hp < 3 else len(convln_parts)):
                    if convln_parts:
                        convln_parts.pop(0)()
            if g == 1:
                for r_ in rVB + rTMP + [rCO]:
                    inherit(rGB, r_)
                ln_steps = make_ln([1, 2, 3, 4], 1)
            for nb in range(8):
                s = wnext("out")
                vo = wview(WS[s], KC, 256, 0)
                for tg in range(4):
                    t = 1 + g * 4 + tg
                    b = genbank()
                    mmgroup(b, [(PB[b][:, 0:256], YT[:, kk, tg * 128:(tg + 1) * 128], vo[:, kk, :], kk == 0, kk == KC - 1)
                                for kk in range(KC)], reads=[rYT, rWS[s]])
                    src, rs = rtile(t)
                    o = src[:, nb * 256:(nb + 1) * 256]
                    op("dve", lambda e, b=b, o=o: e.scalar_tensor_tensor(out=o, in0=PB[b][:, 0:256], scalar=inv_a, in1=o,
                                                                          op0=ALU.mult, op1=ALU.add),
                       reads=[pbres[b]], writes=[rs])
                    op("dve", lambda e, o=o, t=t, nb=nb: e.bn_stats(out=STT[:, t, nb * 6:(nb + 1) * 6], in_=o),
                       reads=[rs], writes=[rSTT[t]])
                if g == 1 and ln_steps:
                    ln_steps.pop(0)()
        run_steps(ln_steps)
        inherit(rHT, rYT, rCO)

        ln_steps = make_ln([5, 6, 7, 8], 1, load=False)
        ln_steps = ffn(1, list(range(1, 9)), 2, split_first=True, tmps=mix_tmps, pre_steps=ln_steps, final=True)
        run_steps(ln_steps)

        P.emit("sp", lambda e: e.nop(), [otoks[-1]])

        P.build(nc, sems)
    return nc


_NC_CACHE = {}


def kernel(x, ffn1_w_gate_up, ffn1_w_down, ln1_g, ln1_b, mix_w_in, conv_w, conv_b, conv_ln_g, conv_ln_b,
           sg_ln_g, sg_ln_b, sg_w, sg_b, mix_w_out, ln2_g, ln2_b, ffn2_w_gate_up, ffn2_w_down, ln3_g, ln3_b):
    f = lambda a: np.ascontiguousarray(np.asarray(a, dtype=np.float32))
    x = f(x)
    common = {
        "w_gu1": f(ffn1_w_gate_up)[0], "w_gu2": f(ffn2_w_gate_up)[0],
        "w_dn1": f(ffn1_w_down)[0], "w_dn2": f(ffn2_w_down)[0],
        "w_in": f(mix_w_in)[0], "w_out": f(mix_w_out)[0],
        "ln1_g": f(ln1_g).reshape(1, D), "ln1_b": f(ln1_b).reshape(1, D),
        "ln2_g": f(ln2_g).reshape(1, D), "ln2_b": f(ln2_b).reshape(1, D),
        "ln3_g": f(ln3_g).reshape(1, D), "ln3_b": f(ln3_b).reshape(1, D),
        "cw": np.ascontiguousarray(f(conv_w)[0].reshape(31, 8, 128).transpose(2, 1, 0).reshape(128, 8 * 31)),
        "cvec": np.ascontiguousarray(np.concatenate([f(conv_b)[0].reshape(8, 128).T, f(conv_ln_g)[0].reshape(8, 128).T,
                                                     f(conv_ln_b)[0].reshape(8, 128).T], axis=1)),
        "sgT": np.ascontiguousarray(np.concatenate([f(sg_ln_g).reshape(8, 128).T, f(sg_ln_b).reshape(8, 128).T], axis=1)),
        "sgw": f(sg_w)[0], "sgbias": f(sg_b).reshape(1, 1024),
        "ident": np.eye(128, dtype=np.float32),
        "maskT": np.triu(np.ones((128, 128), dtype=np.float32)),
    }
    in_maps = []
    for c in range(NCORES):
        b, s0 = c // 4, (c % 4) * TOK
        m = dict(common)
        m["x_main"] = np.ascontiguousarray(x[b, s0:s0 + TOK])
        if s0 > 0:
            m["x_halo"] = np.ascontiguousarray(x[b, s0 - HALO:s0])
            m["hmask"] = np.ones((128, 1), np.float32)
        else:
            m["x_halo"] = np.zeros((HALO, D), np.float32)
            m["hmask"] = np.zeros((128, 1), np.float32)
        in_maps.append(m)
    if "nc" not in _NC_CACHE:
        _NC_CACHE["nc"] = build_program()
    nc = _NC_CACHE["nc"]
    res = run_bass_kernel_spmd(nc, in_maps, core_ids=list(range(NCORES)))
    out = np.empty((2, 4096, D), np.float32)
    for c in range(NCORES):
        b, s0 = c // 4, (c % 4) * TOK
        out[b, s0:s0 + TOK] = res.results[c]["out"]
    return out
```
